# Optimizing a Trainium2 kernel written in Bass

```python
import jax, jax.numpy as jnp
from jax import lax
import numpy as np

D_MODEL = 1024
BATCH = 4
SEQ = 8192
DEPTH = 2

GRID_W = 64
CTX_LEN = 256
N_MIXERS = 2
CHUNK = 128
D_FF = 2816
FFN_RES = 0.5
NORM_EPS = 1e-6
N_MOD = 9

M_HEADS = 8
M_DK = D_MODEL // 16
M_DV = D_MODEL // M_HEADS
M_QK = M_HEADS * M_DK
M_V = M_HEADS * M_DV
CONV_W = 5
M_IN = 2 * M_QK + 2 * M_V + 4 * M_HEADS
M_INIT = -1e30

R_HEADS = 4
R_DK = D_MODEL // R_HEADS
R_DV = 2 * R_DK
R_QK = R_HEADS * R_DK
R_V = R_HEADS * R_DV
R_IN = 2 * R_QK + 2 * R_V
ROPE_BASE = 10000.0

N_A = (DEPTH + 1) // 2
N_B = DEPTH // 2

kernel_name = "hybrid_mlstm_retention_macaron_dit"

F32 = jnp.float32


def rmsnorm(h, g):
    hf = h.astype(F32)
    y = hf * lax.rsqrt(jnp.mean(hf * hf, axis=-1, keepdims=True) + NORM_EPS)
    return (y * g.astype(F32)).astype(h.dtype)


def modnorm(h, g, mod, j):
    return rmsnorm(h, g) * (1 + mod[:, :, 3 * j + 1]) + mod[:, :, 3 * j]


def swiglu(h, w13, w2):
    a, b = jnp.split(h @ w13, 2, axis=-1)
    return (jax.nn.silu(a) * b) @ w2


def half_ffn(h, mod, g, w13, w2, j):
    return h + FFN_RES * mod[:, :, 3 * j + 2] * swiglu(modnorm(h, g, mod, j), w13, w2)


def to_heads(a, n_heads):
    B, T, _ = a.shape
    return a.reshape(B, T, n_heads, -1).transpose(0, 2, 1, 3)


def head_norm(o):
    mu = jnp.mean(o, axis=-1, keepdims=True)
    var = jnp.mean(jnp.square(o - mu), axis=-1, keepdims=True)
    o = (o - mu) * lax.rsqrt(var + NORM_EPS)
    B, H, T, d = o.shape
    return o.transpose(0, 2, 1, 3).reshape(B, T, H * d)


def to_chunks(a):
    B, H, T = a.shape[:3]
    a = a.reshape(B, H, T // CHUNK, CHUNK, *a.shape[3:])
    return jnp.moveaxis(a, 2, 0)


def from_chunks(a):
    a = jnp.moveaxis(a, 0, 2)
    return a.reshape(a.shape[0], a.shape[1], -1, *a.shape[4:])


def centred_dwconv(a, w):
    C = a.shape[-1]
    return lax.conv_general_dilated(
        a, w[:, None, :].astype(a.dtype), window_strides=(1,),
        padding=[(CONV_W // 2, CONV_W // 2)],
        dimension_numbers=("NWC", "WIO", "NWC"), feature_group_count=C)


def run_bidirectional(scan_fw, scan_bw, ctx_fw, ctx_bw, lat_fw, lat_bw, init):
    flip = lambda xs: tuple(jnp.flip(a, axis=2) for a in xs)
    oc_f, st_f = scan_fw(ctx_fw, init)
    ox_f, _ = scan_fw(lat_fw, st_f)
    oc_b, st_b = scan_bw(flip(ctx_bw), init)
    ox_b, _ = scan_bw(flip(lat_bw), st_b)
    return oc_f + jnp.flip(oc_b, axis=2), ox_f + jnp.flip(ox_b, axis=2)


def mlstm_scan(inputs, state):
    tri = jnp.tril(jnp.ones((CHUNK, CHUNK), dtype=bool))

    def step(carry, inp):
        C, n, m = carry
        q, k, v, ig, lf = inp
        b = jnp.cumsum(lf, axis=-1)
        a = b + m[..., None]
        d = jnp.where(tri, b[..., :, None] - b[..., None, :] + ig[..., None, :], -jnp.inf)
        m_t = jnp.maximum(a, jnp.max(d, axis=-1))
        w_inter = jnp.exp(a - m_t)
        s = jnp.einsum("bhtd,bhsd->bhts", q, k) * jnp.exp(d - m_t[..., None])
        num = w_inter[..., None] * jnp.einsum("bhtd,bhde->bhte", q, C) + jnp.einsum("bhts,bhse->bhte", s, v)
        den = w_inter * jnp.einsum("bhtd,bhd->bht", q, n) + jnp.sum(s, axis=-1)
        h = num / jnp.maximum(jnp.abs(den), jnp.exp(-m_t))[..., None]
        g_prev = b[..., -1] + m
        g_s = b[..., -1:] - b + ig
        m_new = jnp.maximum(g_prev, jnp.max(g_s, axis=-1))
        w_prev = jnp.exp(g_prev - m_new)
        w_s = jnp.exp(g_s - m_new[..., None])
        C = w_prev[..., None, None] * C + jnp.einsum("bhs,bhsd,bhse->bhde", w_s, k, v)
        n = w_prev[..., None] * n + jnp.einsum("bhs,bhsd->bhd", w_s, k)
        return (C, n, m_new), h

    state, h = lax.scan(step, state, tuple(to_chunks(a) for a in inputs))
    return from_chunks(h), state


def retention_scan(inputs, R, lg):
    pos = jnp.arange(CHUNK, dtype=F32)
    diff = pos[:, None] - pos[None, :]
    dec = jnp.exp(jnp.where(diff >= 0, diff[None] * lg[:, None, None], -jnp.inf))
    xi = jnp.exp((pos[None] + 1.0) * lg[:, None])
    zeta = jnp.exp((CHUNK - 1.0 - pos)[None] * lg[:, None])
    g_chunk = jnp.exp(CHUNK * lg)

    def step(R, inp):
        q, k, v = inp
        s = jnp.einsum("bhtd,bhsd->bhts", q, k) * dec
        o = jnp.einsum("bhts,bhse->bhte", s, v) + jnp.einsum("bhtd,bhde->bhte", q, R) * xi[..., None]
        R = g_chunk[:, None, None] * R + jnp.einsum("bhsd,bhse->bhde", k * zeta[..., None], v)
        return R, o

    R, o = lax.scan(step, R, tuple(to_chunks(a) for a in inputs))
    return from_chunks(o), R


def axial_rope(a, rows):
    n_pairs = a.shape[-1] // 2
    n_f = n_pairs // 2
    inv = jnp.power(ROPE_BASE, -jnp.arange(n_f, dtype=F32) / n_f)
    row = jnp.broadcast_to(jnp.arange(rows, dtype=F32)[:, None], (rows, GRID_W)).reshape(-1)
    col = jnp.broadcast_to(jnp.arange(GRID_W, dtype=F32)[None, :], (rows, GRID_W)).reshape(-1)
    ang = jnp.concatenate([row[:, None] * inv, col[:, None] * inv], axis=-1)
    cos, sin = jnp.cos(ang), jnp.sin(ang)
    ap = a.reshape(*a.shape[:-1], n_pairs, 2)
    ae, ao = ap[..., 0], ap[..., 1]
    return jnp.stack([ae * cos - ao * sin, ae * sin + ao * cos], axis=-1).reshape(a.shape)


def mlstm_mixer(hc, hx, w_in, gate_b, conv_w, norm_g, w_out, with_ctx_out):
    def project(h):
        B, T, _ = h.shape
        u = h @ w_in
        qk, v, o, gates = jnp.split(u, [2 * M_QK, 2 * M_QK + M_V, 2 * M_QK + 2 * M_V], axis=-1)
        qk = jax.nn.silu(centred_dwconv(qk, conv_w))
        q, k = jnp.split(qk, 2, axis=-1)
        q = to_heads(q, M_HEADS).astype(F32)
        k = to_heads(k, M_HEADS).astype(F32) * (M_DK ** -0.5)
        v = to_heads(v, M_HEADS).astype(F32)
        g = (gates + gate_b).astype(F32).reshape(B, T, 4, M_HEADS).transpose(2, 0, 3, 1)
        fw = (q, k, v, g[0], jax.nn.log_sigmoid(g[1]))
        bw = (q, k, v, g[2], jax.nn.log_sigmoid(g[3]))
        return fw, bw, o

    c_fw, c_bw, oc = project(hc)
    x_fw, x_bw, ox = project(hx)
    B = hx.shape[0]
    init = (jnp.zeros((B, M_HEADS, M_DK, M_DV), F32), jnp.zeros((B, M_HEADS, M_DK), F32),
            jnp.full((B, M_HEADS), M_INIT, F32))
    hc_sum, hx_sum = run_bidirectional(mlstm_scan, mlstm_scan, c_fw, c_bw, x_fw, x_bw, init)

    def out(hs, o, ref):
        y = head_norm(hs) * norm_g.astype(F32) * jax.nn.sigmoid(o.astype(F32))
        return y.astype(ref.dtype) @ w_out

    yx = out(hx_sum, ox, hx)
    yc = out(hc_sum, oc, hc) if with_ctx_out else None
    return yx, yc


def retention_mixer(hc, hx, rows, w_in, decay_logit, norm_g, w_out, with_ctx_out):
    def project(h, rope):
        u = h @ w_in
        q, k, v, g = jnp.split(u, [R_QK, 2 * R_QK, 2 * R_QK + R_V], axis=-1)
        q = to_heads(q, R_HEADS).astype(F32)
        k = to_heads(k, R_HEADS).astype(F32) * (R_DK ** -0.5)
        v = to_heads(v, R_HEADS).astype(F32)
        if rope:
            q, k = axial_rope(q, rows), axial_rope(k, rows)
        return (q, k, v), g

    lg = jax.nn.log_sigmoid(decay_logit.astype(F32))
    c_in, gc = project(hc, False)
    x_in, gx = project(hx, True)
    B = hx.shape[0]
    init = jnp.zeros((B, R_HEADS, R_DK, R_DV), F32)
    scan_fw = lambda inp, st: retention_scan(inp, st, lg[0])
    scan_bw = lambda inp, st: retention_scan(inp, st, lg[1])
    oc_sum, ox_sum = run_bidirectional(scan_fw, scan_bw, c_in, c_in, x_in, x_in, init)

    def out(o, g, ref):
        y = head_norm(o) * norm_g.astype(F32) * jax.nn.silu(g.astype(F32))
        return y.astype(ref.dtype) @ w_out

    yx = out(ox_sum, gx, hx)
    yc = out(oc_sum, gc, hc) if with_ctx_out else None
    return yx, yc


def setup_inputs(seed: int = 0) -> dict:
    key = jax.random.key(seed)
    ks = jax.random.split(key, 20)
    nrm = lambda k, shape, s: jax.random.normal(k, shape, F32) * s
    x = nrm(ks[0], (BATCH, SEQ, D_MODEL), 1.0)
    c = nrm(ks[1], (BATCH, D_MODEL), 1.0)
    ctx = nrm(ks[2], (BATCH, CTX_LEN, D_MODEL), 1.0)
    c_ctx = nrm(ks[3], (D_MODEL,), 1.0)
    mod_w = nrm(ks[4], (DEPTH, D_MODEL, N_MOD * D_MODEL), 0.5 * D_MODEL ** -0.5)
    mod_b = nrm(ks[5], (DEPTH, N_MOD * D_MODEL), 0.01)
    norm_g = 1.0 + nrm(ks[6], (DEPTH, 3, D_MODEL), 0.01)
    ffn_w13 = nrm(ks[7], (DEPTH, 2, D_MODEL, 2 * D_FF), D_MODEL ** -0.5)
    ffn_w2 = nrm(ks[8], (DEPTH, 2, D_FF, D_MODEL), D_FF ** -0.5)
    m_w_in = nrm(ks[9], (N_A, D_MODEL, M_IN), D_MODEL ** -0.5)
    f_bias = jnp.linspace(3.0, 6.0, M_HEADS, dtype=F32)
    zero_h = jnp.zeros((M_HEADS,), F32)
    m_gate_b = jnp.concatenate([zero_h, f_bias, zero_h, f_bias]) + nrm(ks[10], (N_A, 4 * M_HEADS), 0.1)
    m_conv_w = nrm(ks[11], (N_A, CONV_W, 2 * M_QK), CONV_W ** -0.5)
    m_norm_g = 1.0 + nrm(ks[12], (N_A, M_V), 0.01)
    m_w_out = nrm(ks[13], (N_A, M_V, D_MODEL), M_V ** -0.5)
    r_w_in = nrm(ks[14], (N_B, D_MODEL, R_IN), D_MODEL ** -0.5)
    decay0 = jnp.log(jnp.exp2(5.0 + jnp.arange(R_HEADS, dtype=F32)) - 1.0)
    r_decay = decay0 + nrm(ks[15], (N_B, 2, R_HEADS), 0.05)
    r_norm_g = 1.0 + nrm(ks[16], (N_B, R_V), 0.01)
    r_w_out = nrm(ks[17], (N_B, R_V, D_MODEL), R_V ** -0.5)
    final_g = 1.0 + nrm(ks[18], (D_MODEL,), 0.01)
    return {"x": x, "c": c, "ctx": ctx, "c_ctx": c_ctx, "mod_w": mod_w, "mod_b": mod_b,
            "norm_g": norm_g, "ffn_w13": ffn_w13, "ffn_w2": ffn_w2, "m_w_in": m_w_in,
            "m_gate_b": m_gate_b, "m_conv_w": m_conv_w, "m_norm_g": m_norm_g, "m_w_out": m_w_out,
            "r_w_in": r_w_in, "r_decay": r_decay, "r_norm_g": r_norm_g, "r_w_out": r_w_out,
            "final_g": final_g}


def reference(x, c, ctx, c_ctx, mod_w, mod_b, norm_g, ffn_w13, ffn_w2, m_w_in, m_gate_b, m_conv_w,
              m_norm_g, m_w_out, r_w_in, r_decay, r_norm_g, r_w_out, final_g):
    B, T, D = x.shape
    rows = T // GRID_W
    sc = jax.nn.silu(c)
    scc = jax.nn.silu(c_ctx)
    for i in range(DEPTH):
        mod_x = (sc @ mod_w[i] + mod_b[i]).reshape(B, 1, N_MOD, D)
        mod_c = (scc @ mod_w[i] + mod_b[i]).reshape(1, 1, N_MOD, D)
        last = i == DEPTH - 1
        j = i // N_MIXERS
        x = half_ffn(x, mod_x, norm_g[i, 0], ffn_w13[i, 0], ffn_w2[i, 0], 0)
        ctx = half_ffn(ctx, mod_c, norm_g[i, 0], ffn_w13[i, 0], ffn_w2[i, 0], 0)
        hx = modnorm(x, norm_g[i, 1], mod_x, 1)
        hc = modnorm(ctx, norm_g[i, 1], mod_c, 1)
        if i % N_MIXERS == 0:
            yx, yc = mlstm_mixer(hc, hx, m_w_in[j], m_gate_b[j], m_conv_w[j], m_norm_g[j], m_w_out[j], not last)
        else:
            yx, yc = retention_mixer(hc, hx, rows, r_w_in[j], r_decay[j], r_norm_g[j], r_w_out[j], not last)
        x = x + mod_x[:, :, 5] * yx
        x = half_ffn(x, mod_x, norm_g[i, 2], ffn_w13[i, 1], ffn_w2[i, 1], 2)
        if not last:
            ctx = ctx + mod_c[:, :, 5] * yc
            ctx = half_ffn(ctx, mod_c, norm_g[i, 2], ffn_w13[i, 1], ffn_w2[i, 1], 2)
    return rmsnorm(x, final_g)
```

```python
import numpy as np
from contextlib import ExitStack
import concourse.bass as bass
import concourse.mybir as mybir
from concourse.bass_utils import run_bass_kernel_spmd

F32 = mybir.dt.float32
BF16 = mybir.dt.bfloat16
AF = mybir.ActivationFunctionType
ALU = mybir.AluOpType
AX = mybir.AxisListType

D = 1024
KC = 8
CTX = 256
SEQ = 8192
NT = CTX + SEQ
NCH = NT // 128


def set_seq(n):
    global SEQ, NT, NCH
    SEQ = n
    NT = CTX + SEQ
    NCH = NT // 128

DFF = 2816
HC = DFF // 128
NMOD = 9
EPS = 1e-6
M_IN = 3104
R_IN = 6144
N_CORES = 4


class Reg:
    __slots__ = ("w", "r")

    def __init__(self):
        self.w = {}
        self.r = {}


def _merge(d, s):
    for k, v in s.items():
        if d.get(k, 0) < v:
            d[k] = v


class Sy:
    def __init__(self, nc):
        self.nc = nc
        self.sems = {}
        self.nsem = 0
        self.eng = {}
        for n in ("tensor", "vector", "scalar", "gpsimd", "sync"):
            self.eng[n] = dict(h=getattr(nc, n), sem=self._new(), cnt=0, waited={})
        self.dq = {}
        for n in ("sync", "gpsimd", "scalar"):
            self.dq[n] = dict(sems=[self._new() for _ in range(12)], vals=[0] * 12, idx=0)

    def _new(self):
        i = self.nsem
        self.nsem += 1
        self.sems[i] = self.nc.alloc_semaphore(f"sem{i}")
        return i

    def _wait(self, e, deps):
        for s, v in deps.items():
            if e["waited"].get(s, 0) < v:
                e["h"].wait_ge(self.sems[s], v)
                e["waited"][s] = v

    def _deps(self, reads, writes, own=None):
        deps = {}
        for r in reads:
            _merge(deps, r.w)
        for w in writes:
            for src in (w.w, w.r):
                for k, v in src.items():
                    if deps.get(k, 0) < v:
                        deps[k] = v
        return deps

    def _commit(self, tok, reads, writes):
        for w in writes:
            w.w = {tok[0]: tok[1]}
            w.r = {}
        for r in reads:
            if r.r.get(tok[0], 0) < tok[1]:
                r.r[tok[0]] = tok[1]

    def op(self, en, fn, reads=(), writes=(), inc=True):
        e = self.eng[en]
        if (not inc) and e["cnt"] >= 29900:
            e["sem"] = self._new()
            e["cnt"] = 0
        own = e["sem"] if en != "tensor" or True else None
        self._wait(e, self._deps(reads, writes, own=own))
        ins = fn(e["h"])
        if inc:
            if e["cnt"] >= 30000:
                e["sem"] = self._new()
                e["cnt"] = 0
            e["cnt"] += 1
            ins.then_inc(self.sems[e["sem"]], 1)
            tok = (e["sem"], e["cnt"])
        else:
            if e["cnt"] >= 29900:
                e["sem"] = self._new()
                e["cnt"] = 0
            tok = (e["sem"], e["cnt"] + 1)
        self._commit(tok, reads, writes)

    def dma(self, qn, out, in_, reads=(), writes=()):
        e = self.eng[qn]
        q = self.dq[qn]
        k = q["idx"]
        q["idx"] = (k + 1) % len(q["sems"])
        if q["vals"][k] >= 30000:
            q["sems"][k] = self._new()
            q["vals"][k] = 0
        deps = self._deps(reads, writes)
        if q["vals"][k] > 0:
            _merge(deps, {q["sems"][k]: q["vals"][k]})
        self._wait(e, deps)
        q["vals"][k] += 16
        e["h"].dma_start(out=out, in_=in_).then_inc(self.sems[q["sems"][k]], 16)
        self._commit((q["sems"][k], q["vals"][k]), reads, writes)

    def mm(self, out, pairs, reads, wreg, tr=False):
        n = len(pairs)
        for i, (l, r) in enumerate(pairs):
            self.op("tensor",
                    lambda t, l=l, r=r, i=i: t.matmul(out, lhsT=l, rhs=r, start=(i == 0), stop=(i == n - 1)),
                    reads=reads if i == 0 else (), writes=(wreg,) if i == 0 else (), inc=(i == n - 1))
        if n > 1:
            e = self.eng["tensor"]
            tok = (e["sem"], e["cnt"])
            wreg.w = {tok[0]: tok[1]}
            for r in reads:
                if r.r.get(tok[0], 0) < tok[1]:
                    r.r[tok[0]] = tok[1]

    def barrier(self):
        deps = {}
        for n, e in self.eng.items():
            if e["cnt"] > 0:
                deps[e["sem"]] = e["cnt"]
        for n, q in self.dq.items():
            for k in range(len(q["sems"])):
                if q["vals"][k] > 0:
                    deps[q["sems"][k]] = q["vals"][k]
        for n, e in self.eng.items():
            self._wait(e, dict(deps))

    def finish(self, regs):
        e = self.eng["sync"]
        deps = {}
        for r in regs:
            _merge(deps, r.w)
        self._wait(e, deps)


def run_pipeline(factories, depth=2, stagger=False):
    live = []
    it = iter(factories)
    done = False
    while True:
        started = 0
        while len(live) < depth and not done and not (stagger and started >= 1):
            try:
                live.append(next(it)())
                started += 1
            except StopIteration:
                done = True
        if not live:
            break
        nxt = []
        for g_ in live:
            try:
                next(g_)
                nxt.append(g_)
            except StopIteration:
                pass
        live = nxt


class Builder:
    def __init__(self, stop_after=None):
        self.stop_after = stop_after
        nc = bass.Bass("TRN2", target_bir_lowering=False)
        self.nc = nc
        self.sy = Sy(nc)
        self.regs = {}
        self.inputs()
        self.consts_and_state()

    def R(self, *key):
        r = self.regs.get(key)
        if r is None:
            r = self.regs[key] = Reg()
        return r

    def din(self, name, shape, dt=F32):
        return self.nc.dram_tensor(name, list(shape), dt, kind="ExternalInput").ap()

    def dscr(self, name, shape, dt):
        return self.nc.dram_tensor(name, list(shape), dt, kind="Internal").ap()

    def sb(self, name, shape, dt=F32):
        return self.nc.alloc_sbuf_tensor(name, list(shape), dt)

    def phase_begin(self):
        self.sy.barrier()
        self._stack = ExitStack()
        self._pn = getattr(self, "_pn", 0) + 1

    def tsb(self, name, shape, dt=F32):
        return self._stack.enter_context(self.nc.sbuf_tensor(f"{name}_p{self._pn}", list(shape), dt))

    def phase_end(self):
        self.sy.barrier()
        self._stack.close()
        self._stack = None

    def inputs(self):
        self.xT_in = self.din("xT", [D, NT])
        self.cT = self.din("cT", [128, KC, 2])
        self.mod_w = self.din("mod_w", [2, D, NMOD * D])
        self.mod_b = self.din("mod_b", [2, 128, NMOD * KC])
        self.norm_g = self.din("norm_g", [128, 2 * 3 * KC])
        self.ffn_w13 = self.din("ffn_w13", [2, 2, D, 2 * DFF])
        self.ffn_w2 = self.din("ffn_w2", [2, 2, DFF, D])
        self.final_g = self.din("final_g", [128, KC])
        self.ident_in = self.din("ident", [128, 128])
        self.outT = self.nc.dram_tensor("outT", [D, SEQ], F32, kind="ExternalOutput").ap()
        self.m_w_in = self.din("m_w_in", [D, M_IN])
        self.m_w_out = self.din("m_w_out", [D, D])
        self.m_gate_b = self.din("m_gate_b", [128, 32])
        self.m_conv_w = self.din("m_conv_w", [128, KC, 5])
        self.m_norm_g = self.din("m_norm_g", [128, D])
        self.masks_in = self.din("masks", [4, 128, 128])
        self.r_w_in = self.din("r_w_in", [D, R_IN])
        self.r_w_out = self.din("r_w_out", [2048, D])
        self.r_decay = self.din("r_decay", [128, 8])
        self.r_norm_g = self.din("r_norm_g", [128, 2048])
        self.r_pos = self.din("r_pos", [128, 4])
        self.rope_cs = self.din("rope_cs", [NT, 2, 128])
        self.r_wins = self.dscr("r_wins", [128, KC, R_IN], BF16)
        self.r_wouts = self.dscr("r_wouts", [128, 16, D], BF16)
        self.rqT = self.dscr("rqT", [NCH, 128, D], BF16)
        self.rkT = self.dscr("rkT", [NCH, 128, D], BF16)
        self.rktok = self.dscr("rktok", [NT, D], BF16)
        self.rvtok = self.dscr("rvtok", [NT, 2048], BF16)
        self.rgtok = self.dscr("rgtok", [NT, 2048], F32)
        self.rofw = self.dscr("rofw", [NT, 2048], F32)
        self.robw = self.dscr("robw", [NT, 2048], F32)
        self.m_wins = self.dscr("m_wins", [128, KC, M_IN], BF16)
        self.m_wouts = self.dscr("m_wouts", [128, KC, D], BF16)
        self.uqkT = self.dscr("uqkT", [D, NT], F32)
        self.qT = self.dscr("qT", [512, NT], BF16)
        self.kT = self.dscr("kT", [512, NT], BF16)
        self.vtok = self.dscr("vtok", [NT, D], BF16)
        self.otok = self.dscr("otok", [NT, D], F32)
        self.hfw = self.dscr("hfw", [NT, D], F32)
        self.hbw = self.dscr("hbw", [NT, D], F32)
        self.xs = self.dscr("xs", [D, NT], F32)
        self.w13s = self.dscr("w13s", [2, 2, 128, KC, 2 * DFF], BF16)
        self.w2s = self.dscr("w2s", [2, 2, 128, HC, D], BF16)

    def consts_and_state(self):
        nc, sy = self.nc, self.sy
        self.ps = [nc.alloc_psum_tensor(f"ps{i}", [128, 512], F32) for i in range(8)]
        self.psr = [self.R("ps", i) for i in range(8)]
        self.ones_bf = self.sb("ones_bf", [128, 128], BF16)
        self.ident_f = self.sb("ident_f", [128, 128], F32)
        self.ident_b = self.sb("ident_b", [128, 128], BF16)
        self.modx = self.sb("modx", [128, 2, NMOD * KC])
        self.modc = self.sb("modc", [128, 2, NMOD * KC])
        self.ng = self.sb("ng", [128, 2 * 3 * KC])
        self.fg = self.sb("fg", [128, KC])
        self.cst = self.R("consts")
        self.masks = self.sb("masks_sb", [128, 4, 128])
        self.negones = self.sb("negones", [128, 128])
        self.ones_col = self.sb("ones_col", [128, 2], BF16)
        self.ws_all = self.sb("ws_all", [128, NCH, 16])
        self.thr_all = self.sb("thr_all", [128, NCH, 16])
        self.ebl_all = self.sb("ebl_all", [128, NCH, 16])
        for m in range(4):
            sy.dma("sync", self.masks[:, m, :], self.masks_in[m], writes=(self.cst,))
        sy.op("vector", lambda v: v.memset(self.negones[:, :], -1.0), writes=(self.cst,))
        sy.op("vector", lambda v: v.memset(self.ones_col[:, :], 1.0), writes=(self.cst,))
        self._mn = dict(sq=[self.sb(f"mn_sq{b}", [128, 512], BF16) for b in range(2)],
                        tmp=[self.sb(f"mn_tmp{b}", [128, 512]) for b in range(2)],
                        rstd=self.sb("mn_rstd", [128, 512]), n=0)
        sy.op("vector", lambda v: v.memset(self.ones_bf[:, :], 1.0 / 1024.0), writes=(self.cst,))
        sy.dma("sync", self.ident_f[:, :], self.ident_in, writes=(self.cst,))
        sy.op("vector", lambda v: v.tensor_copy(out=self.ident_b[:, :], in_=self.ident_f[:, :]),
              reads=(self.cst,), writes=(self.R("identb"),))
        sy.dma("sync", self.ng[:, :], self.norm_g, writes=(self.cst,))
        sy.dma("sync", self.fg[:, :], self.final_g, writes=(self.cst,))

    def precast_ffn(self, i, j):
        sy = self.sy
        for k in range(KC):
            sy.dma("gpsimd", self.w13s[i, j, :, k, :], self.ffn_w13[i, j, k * 128:(k + 1) * 128, :], writes=(self.R("w13s", i, j, k),))
        for k in range(HC):
            sy.dma("gpsimd", self.w2s[i, j, :, k, :], self.ffn_w2[i, j, k * 128:(k + 1) * 128, :], writes=(self.R("w2s", i, j, k),))

    def compute_mod(self, layers=(0, 1)):
        nc, sy = self.nc, self.sy
        self.phase_begin()
        s_raw = self.tsb("s_raw", [128, KC, 2])
        s_act = self.tsb("s_act", [128, KC, 2])
        mb = self.tsb("mb", [128, 2, NMOD * KC])
        r_s = self.R("s_act")
        sy.dma("sync", s_raw[:, :, :], self.cT, writes=(r_s,))
        sy.op("scalar", lambda a: a.activation(out=s_act[:, :, :], in_=s_raw[:, :, :], func=AF.Silu),
              reads=(r_s,), writes=(r_s,))
        r_mb = self.R("mb")
        for i in layers:
            sy.dma("sync", mb[:, i, :], self.mod_b[i], writes=(r_mb,))
        NB = 512
        wbuf = [self.tsb(f"modw{b}", [128, KC, NB]) for b in range(2)]
        rw = [self.R("modw", b) for b in range(2)]
        r_mod = self.R("mod")
        pst = self.ps[0]
        n = 0
        for i in layers:
            for blk in range(NMOD * D // NB):
                b = n % 2
                n += 1
                sy.dma("sync", wbuf[b][:, :, :],
                       self.mod_w[i, :, blk * NB:(blk + 1) * NB].rearrange("(k p) n -> p k n", p=128),
                       writes=(rw[b],))
                for fc in range(NB // 128):
                    col = (blk * NB) // 128 + fc
                    first = (blk == 0 and fc == 0)
                    sy.mm(pst[:, 2 * col:2 * col + 2],
                          [(wbuf[b][:, k, fc * 128:(fc + 1) * 128], s_act[:, k, :]) for k in range(KC)],
                          reads=(rw[b], r_s), wreg=self.psr[0] if first else Reg())
            e = sy.eng["tensor"]
            self.psr[0].w = {e["sem"]: e["cnt"]}
            pv = pst[:, 0:2 * NMOD * KC].rearrange("p (c t) -> p c t", t=2)
            sy.op("vector", lambda v, i=i, pv=pv: v.tensor_tensor(out=self.modx[:, i, :], in0=pv[:, :, 0], in1=mb[:, i, :], op=ALU.add),
                  reads=(self.psr[0], r_mb), writes=(r_mod,))
            sy.op("vector", lambda v, i=i, pv=pv: v.tensor_tensor(out=self.modc[:, i, :], in0=pv[:, :, 1], in1=mb[:, i, :], op=ALU.add),
                  reads=(self.psr[0], r_mb), writes=(r_mod,))
        self.r_mod = r_mod
        self.phase_end()

    def sub_mod(self, i, j, res):
        sy = self.sy
        key = (i, j)
        gs = self.sb(f"gs{i}{j}", [128, 2, KC])
        hg = self.sb(f"hg{i}{j}", [128, 2, KC])
        sh = []
        r = self.R("submod", i, j)
        for v_, m in enumerate((self.modx, self.modc)):
            sc = m[:, i, (3 * j + 1) * KC:(3 * j + 2) * KC]
            gt = m[:, i, (3 * j + 2) * KC:(3 * j + 3) * KC]
            g = self.ng[:, (i * 3 + j) * KC:(i * 3 + j + 1) * KC]
            sy.op("vector", lambda v, v_=v_, sc=sc, g=g: v.scalar_tensor_tensor(
                out=gs[:, v_, :], in0=sc, scalar=1.0, in1=g, op0=ALU.add, op1=ALU.mult),
                reads=(self.r_mod, self.cst), writes=(r,))
            sy.op("vector", lambda v, v_=v_, gt=gt: v.tensor_scalar(
                out=hg[:, v_, :], in0=gt, scalar1=float(res), scalar2=None, op0=ALU.mult),
                reads=(self.r_mod,), writes=(r,))
            sh.append(m[:, i, (3 * j) * KC:(3 * j + 1) * KC])
        return gs, sh, hg, r

    def rms_stats(self, xt, xr, col0, w):
        sy = self.sy
        mn = self._mn
        bank = 7
        for k in range(KC):
            b = mn["n"] % 2
            mn["n"] += 1
            sq = mn["sq"][b]
            sy.op("scalar", lambda a, sq=sq, k=k: a.activation(out=sq[:, 0:w], in_=xt[:, k, col0:col0 + w], func=AF.Square),
                  reads=(xr,), writes=(self.R("mn_sq", b),))
            sy.op("tensor", lambda t, sq=sq, k=k: t.matmul(self.ps[bank][:, 0:w], lhsT=self.ones_bf[:, :], rhs=sq[:, 0:w],
                                                          start=(k == 0), stop=(k == KC - 1)),
                  reads=(self.R("mn_sq", b), self.cst), writes=(self.psr[bank],) if k == 0 else (), inc=True)
        e = sy.eng["tensor"]
        self.psr[bank].w = {e["sem"]: e["cnt"]}
        sy.op("scalar", lambda a: a.activation(out=mn["rstd"][:, 0:w], in_=self.ps[bank][:, 0:w], func=AF.Ln, bias=EPS, scale=1.0),
              reads=(self.psr[bank],), writes=(self.R("mn_rstd"),))
        sy.op("scalar", lambda a: a.activation(out=mn["rstd"][:, 0:w], in_=mn["rstd"][:, 0:w], func=AF.Exp, scale=-0.5),
              reads=(self.R("mn_rstd"),), writes=(self.R("mn_rstd"),))

    def modnorm_piece(self, xt, xr, col0, w, kind, gs, sh, rmod, hT, hr, hcol0):
        sy = self.sy
        v_ = 0 if kind == "x" else 1
        self.rms_stats(xt, xr, col0, w)
        mn = self._mn
        rs_r = self.R("mn_rstd")
        for k in range(KC):
            b = mn["n"] % 2
            mn["n"] += 1
            tmp = mn["tmp"][b]
            sy.op("vector", lambda v, tmp=tmp, k=k: v.scalar_tensor_tensor(
                out=tmp[:, 0:w], in0=xt[:, k, col0:col0 + w], scalar=gs[:, v_, k:k + 1], in1=mn["rstd"][:, 0:w],
                op0=ALU.mult, op1=ALU.mult),
                reads=(xr, rs_r, rmod), writes=(self.R("mn_tmp", b),))
            sy.op("scalar", lambda a, tmp=tmp, k=k: a.activation(
                out=hT[:, k, hcol0:hcol0 + w], in_=tmp[:, 0:w], func=AF.Identity, bias=sh[v_][:, k:k + 1], scale=1.0),
                reads=(self.R("mn_tmp", b), self.r_mod), writes=(hr,))

    def supertiles(self):
        sts = [[(0, 256, "c")]]
        for s in range(SEQ // 1024):
            b = 256 + s * 1024
            sts.append([(b, 512, "x"), (b + 512, 512, "x")])
        return sts[:getattr(self, "dbg_nst", 99)]

    def ffn(self, i, j, src, final=False):
        nc, sy = self.nc, self.sy
        jm = 0 if j == 0 else 2
        gs, sh, hg, rmod = self.sub_mod(i, jm, 0.5)
        self.phase_begin()
        xt = self.tsb("f_xt", [128, KC, 1024])
        hT = [self.tsb(f"f_hT{b}", [128, KC, 1024], BF16) for b in range(2)]
        g = self.tsb("f_g", [128, HC, 1024], BF16)
        w13 = [self.tsb(f"f_w13_{b}", [128, KC, 512], BF16) for b in range(2)]
        w2q = [self.tsb(f"f_w2_{b}", [128, HC, 256], BF16) for b in range(2)]
        sab = [self.tsb(f"f_sa{b}", [128, 512]) for b in range(2)]
        xr = [self.tsb(f"f_xr{b}", [128, 512]) for b in range(3)]
        r_xt, r_g = Reg(), Reg()
        rw2q = [Reg(), Reg()]
        r_hT = [Reg(), Reg()]
        rw13 = [Reg(), Reg()]
        rsa_ = [Reg(), Reg()]
        r_xr = [Reg(), Reg(), Reg()]
        w13rs = tuple(self.R("w13s", i, j, k) for k in range(KC))
        w2rs = tuple(self.R("w2s", i, j, k) for k in range(HC))
        sts = self.supertiles()
        cnt = dict(n13=0, nsa=0, nps=0, nys=0, nxr=0, nw2=0)

        def load_w2(oq):
            bq = cnt["nw2"] % 2
            cnt["nw2"] += 1
            sy.dma("sync", w2q[bq][:, :, :], self.w2s[i, j, :, :, oq * 256:(oq + 1) * 256], reads=w2rs, writes=(rw2q[bq],))
            return bq

        def gen_norm(si):
            st = sts[si]
            base = st[0][0]
            tot = sum(p[1] for p in st)
            for k in range(KC):
                sy.dma("sync", xt[:, k, 0:tot], src[k * 128:(k + 1) * 128, base:base + tot], writes=(r_xt,))
            yield
            for (c0, w, kind) in st:
                v_ = 0 if kind == "x" else 1
                lo = c0 - base
                mn = self._mn
                for k in range(KC):
                    b = mn["n"] % 2
                    mn["n"] += 1
                    sq = mn["sq"][b]
                    sy.op("scalar", lambda a, sq=sq, k=k: a.activation(out=sq[:, 0:w], in_=xt[:, k, lo:lo + w], func=AF.Square),
                          reads=(r_xt,), writes=(self.R("mn_sq", b),))
                    sy.op("tensor", lambda t, sq=sq, k=k: t.matmul(self.ps[7][:, 0:w], lhsT=self.ones_bf[:, :], rhs=sq[:, 0:w],
                                                                  start=(k == 0), stop=(k == KC - 1)),
                          reads=(self.R("mn_sq", b), self.cst), writes=(self.psr[7],) if k == 0 else (), inc=True)
                    if k % 2 == 1:
                        yield
                e = sy.eng["tensor"]
                self.psr[7].w = {e["sem"]: e["cnt"]}
                sy.op("scalar", lambda a: a.activation(out=mn["rstd"][:, 0:w], in_=self.ps[7][:, 0:w], func=AF.Ln, bias=EPS, scale=1.0),
                      reads=(self.psr[7],), writes=(self.R("mn_rstd"),))
                sy.op("scalar", lambda a: a.activation(out=mn["rstd"][:, 0:w], in_=mn["rstd"][:, 0:w], func=AF.Exp, scale=-0.5),
                      reads=(self.R("mn_rstd"),), writes=(self.R("mn_rstd"),))
                yield
                for k in range(KC):
                    b = mn["n"] % 2
                    mn["n"] += 1
                    tmp = mn["tmp"][b]
                    sy.op("vector", lambda v, tmp=tmp, k=k: v.scalar_tensor_tensor(
                        out=tmp[:, 0:w], in0=xt[:, k, lo:lo + w], scalar=gs[:, v_, k:k + 1], in1=mn["rstd"][:, 0:w],
                        op0=ALU.mult, op1=ALU.mult),
                        reads=(r_xt, self.R("mn_rstd"), rmod), writes=(self.R("mn_tmp", b),))
                    sy.op("scalar", lambda a, tmp=tmp, k=k: a.activation(
                        out=hT[si % 2][:, k, lo:lo + w], in_=tmp[:, 0:w], func=AF.Identity, bias=sh[v_][:, k:k + 1], scale=1.0),
                        reads=(self.R("mn_tmp", b), self.r_mod), writes=(r_hT[si % 2],))
                    if k % 2 == 1:
                        yield

        def gen_main(si):
            st = sts[si]
            base = st[0][0]
            h_ = hT[si % 2]
            rh = r_hT[si % 2]
            nxt_bq = load_w2(0)
            for hb in range(11):
                bb = cnt["n13"] % 2
                cnt["n13"] += 1
                wt = w13[bb]
                rw = rw13[bb]
                for half in range(2):
                    c_ = half * DFF + hb * 256
                    sy.dma("sync", wt[:, :, half * 256:(half + 1) * 256], self.w13s[i, j, :, :, c_:c_ + 256],
                           reads=w13rs, writes=(rw,))
                for (c0, w, kind) in st:
                    lo = c0 - base
                    for h2 in range(2):
                        pa = 2 * (cnt["nps"] % 2)
                        cnt["nps"] += 1
                        pb = pa + 1
                        sy.mm(self.ps[pa][:, 0:w], [(wt[:, k, h2 * 128:(h2 + 1) * 128], h_[:, k, lo:lo + w]) for k in range(KC)],
                              reads=(rw, rh), wreg=self.psr[pa])
                        sy.mm(self.ps[pb][:, 0:w], [(wt[:, k, 256 + h2 * 128:256 + (h2 + 1) * 128], h_[:, k, lo:lo + w]) for k in range(KC)],
                              reads=(rw, rh), wreg=self.psr[pb])
                        sb_ = cnt["nsa"] % 2
                        cnt["nsa"] += 1
                        sa = sab[sb_]
                        rsa = rsa_[sb_]
                        sy.op("scalar", lambda a, sa=sa, pa=pa, w=w: a.activation(out=sa[:, 0:w], in_=self.ps[pa][:, 0:w], func=AF.Silu),
                              reads=(self.psr[pa],), writes=(rsa,))
                        hc = hb * 2 + h2
                        sy.op("vector", lambda v, sa=sa, pb=pb, hc=hc, lo=lo, w=w: v.tensor_tensor(
                            out=g[:, hc, lo:lo + w], in0=sa[:, 0:w], in1=self.ps[pb][:, 0:w], op=ALU.mult),
                            reads=(rsa, self.psr[pb]), writes=(r_g,))
                yield
            for oq in range(4):
                bq = nxt_bq
                if oq + 1 < 4:
                    nxt_bq = load_w2(oq + 1)
                w2 = w2q[bq]
                rw2 = rw2q[bq]
                for o2 in range(2):
                    oc = oq * 2 + o2
                    for (c0, w, kind) in st:
                        lo = c0 - base
                        v_ = 0 if kind == "x" else 1
                        xb = cnt["nxr"] % 3
                        cnt["nxr"] += 1
                        sy.dma("sync", xr[xb][:, 0:w], src[oc * 128:(oc + 1) * 128, c0:c0 + w], writes=(r_xr[xb],))
                        py = 4 + (cnt["nys"] % 2)
                        cnt["nys"] += 1
                        sy.mm(self.ps[py][:, 0:w], [(w2[:, k, o2 * 128:(o2 + 1) * 128], g[:, k, lo:lo + w]) for k in range(HC)],
                              reads=(rw2, r_g), wreg=self.psr[py])
                        sy.op("vector", lambda v, py=py, oc=oc, v_=v_, w=w, xb=xb: v.scalar_tensor_tensor(
                            out=xr[xb][:, 0:w], in0=self.ps[py][:, 0:w], scalar=hg[:, v_, oc:oc + 1], in1=xr[xb][:, 0:w],
                            op0=ALU.mult, op1=ALU.add),
                            reads=(self.psr[py], rmod, r_xr[xb]), writes=(r_xr[xb],))
                        sy.dma("scalar", self.xs[oc * 128:(oc + 1) * 128, c0:c0 + w], xr[xb][:, 0:w], reads=(r_xr[xb],), writes=(Reg(),))
                    yield

        for _ in gen_norm(0):
            pass
        for si in range(len(sts)):
            ga = gen_main(si)
            gb = gen_norm(si + 1) if si + 1 < len(sts) else iter(())
            a_alive = b_alive = True
            while a_alive or b_alive:
                if a_alive:
                    try:
                        next(ga)
                    except StopIteration:
                        a_alive = False
                for _ in range(2):
                    if b_alive:
                        try:
                            next(gb)
                        except StopIteration:
                            b_alive = False
        self.phase_end()
        if final:
            self.final_norm()

    def final_norm(self):
        sy = self.sy
        self.phase_begin()
        xt = [self.tsb(f"n_xt{b}", [128, KC, 512]) for b in range(2)]
        r_xt = [Reg(), Reg()]
        r_out = self.R("outT")
        mn = self._mn
        w = 512

        def body(pi):
            b = pi % 2
            c0 = CTX + pi * 512
            for k in range(KC):
                sy.dma("sync", xt[b][:, k, :], self.xs[k * 128:(k + 1) * 128, c0:c0 + w], writes=(r_xt[b],))
            yield
            self.rms_stats(xt[b], r_xt[b], 0, w)
            yield
            for k in range(KC):
                sy.op("vector", lambda v, k=k, b=b: v.scalar_tensor_tensor(
                    out=xt[b][:, k, :], in0=xt[b][:, k, :], scalar=self.fg[:, k:k + 1], in1=mn["rstd"][:, 0:w],
                    op0=ALU.mult, op1=ALU.mult),
                    reads=(r_xt[b], self.R("mn_rstd"), self.cst), writes=(r_xt[b],))
            for k in range(KC):
                sy.dma("scalar", self.outT[k * 128:(k + 1) * 128, c0 - CTX:c0 - CTX + w], xt[b][:, k, :], reads=(r_xt[b],), writes=(r_out,))

        run_pipeline([(lambda pi=pi: body(pi)) for pi in range(SEQ // 512)], depth=2, stagger=True)
        self.phase_end()

    def precast_mlstm(self):
        sy = self.sy
        r = self.R("m_wins")
        for k in range(KC):
            sy.dma("gpsimd", self.m_wins[:, k, :], self.m_w_in[k * 128:(k + 1) * 128, :], writes=(r,))
        r2 = self.R("m_wouts")
        for k in range(KC):
            sy.dma("gpsimd", self.m_wouts[:, k, :], self.m_w_out[k * 128:(k + 1) * 128, :], writes=(r2,))

    def pieces(self):
        ps_ = [(0, 256, "c")]
        for s in range(SEQ // 512):
            ps_.append((256 + s * 512, 512, "x"))
        return ps_

    def mlstm_inproj(self, i):
        sy = self.sy
        gs, sh, hg, rmod = self.sub_mod(i, 1, 1.0)
        self.m_hg = hg
        self.m_rmod = rmod
        self.phase_begin()
        win = self.tsb("m_win", [128, KC, M_IN], BF16)
        xt = self.tsb("m_xt", [128, KC, 512])
        hT = self.tsb("m_hT", [128, KC, 512], BF16)
        stq = [self.tsb(f"m_stq{b}", [128, 512]) for b in range(2)]
        stv = [self.tsb(f"m_stv{b}", [128, D], BF16) for b in range(2)]
        sto = [self.tsb(f"m_sto{b}", [128, D]) for b in range(2)]
        gb = self.tsb("m_gb", [128, 32])
        gpre = self.tsb("m_gpre", [128, 4, 8])
        lt = self.tsb("m_lt", [128, 2, 8])
        dtmp = self.tsb("m_dtmp", [128, 2, 8])
        r_win, r_xt, r_hT = Reg(), Reg(), Reg()
        r_stq, r_stv, r_sto = [Reg(), Reg()], [Reg(), Reg()], [Reg(), Reg()]
        r_gb, r_gpre, r_lt, r_dt = Reg(), Reg(), Reg(), Reg()
        r_xs = self.R("xT", id(self.xs))
        r_uqk, r_v, r_o, r_gate = self.R("uqkT"), self.R("vtok"), self.R("otok"), self.R("gates")
        for k in range(KC):
            sy.dma("sync", win[:, k, :], self.m_wins[:, k, :], reads=(self.R("m_wins"),), writes=(r_win,))
        sy.dma("sync", gb[:, :], self.m_gate_b, writes=(r_gb,))
        nq = nv = 0
        for (c0, w, kind) in self.pieces():
            for k in range(KC):
                sy.dma("sync", xt[:, k, 0:w], self.xs[k * 128:(k + 1) * 128, c0:c0 + w], reads=(r_xs,), writes=(r_xt,))
            self.modnorm_piece(xt, r_xt, 0, w, kind, gs, sh, rmod, hT, r_hT, 0)
            for oc in range(KC):
                pb = nq % 2
                b = nq % 2
                nq += 1
                sy.mm(self.ps[pb][:, 0:w], [(win[:, k, oc * 128:(oc + 1) * 128], hT[:, k, 0:w]) for k in range(KC)],
                      reads=(r_win, r_hT), wreg=self.psr[pb])
                sy.op("scalar", lambda a, b=b, pb=pb, w=w: a.copy(out=stq[b][:, 0:w], in_=self.ps[pb][:, 0:w]),
                      reads=(self.psr[pb],), writes=(r_stq[b],))
                sy.dma("gpsimd", self.uqkT[oc * 128:(oc + 1) * 128, c0:c0 + w], stq[b][:, 0:w], reads=(r_stq[b],), writes=(r_uqk,))
            def body(t4, c0=c0, w=w):
                nonlocal nv
                c = c0 // 128 + t4
                tsl = slice(t4 * 128, (t4 + 1) * 128)
                b = nv % 2
                nv += 1
                for n2 in range(2):
                    pb = 2 + n2
                    sy.mm(self.ps[pb][:, :], [(hT[:, k, tsl], win[:, k, 1024 + n2 * 512:1024 + (n2 + 1) * 512]) for k in range(KC)],
                          reads=(r_win, r_hT), wreg=self.psr[pb])
                    sy.op("vector", lambda v, b=b, pb=pb, n2=n2: v.tensor_copy(out=stv[b][:, n2 * 512:(n2 + 1) * 512], in_=self.ps[pb][:, :]),
                          reads=(self.psr[pb],), writes=(r_stv[b],))
                sy.dma("gpsimd", self.vtok[c * 128:(c + 1) * 128, :], stv[b][:, :], reads=(r_stv[b],), writes=(r_v,))
                yield
                for n2 in range(2):
                    pb = 4 + n2
                    sy.mm(self.ps[pb][:, :], [(hT[:, k, tsl], win[:, k, 2048 + n2 * 512:2048 + (n2 + 1) * 512]) for k in range(KC)],
                          reads=(r_win, r_hT), wreg=self.psr[pb])
                    sy.op("scalar", lambda a, b=b, pb=pb, n2=n2: a.copy(out=sto[b][:, n2 * 512:(n2 + 1) * 512], in_=self.ps[pb][:, :]),
                          reads=(self.psr[pb],), writes=(r_sto[b],))
                sy.dma("gpsimd", self.otok[c * 128:(c + 1) * 128, :], sto[b][:, :], reads=(r_sto[b],), writes=(r_o,))
                yield
                pg = 6
                sy.mm(self.ps[pg][:, 0:32], [(hT[:, k, tsl], win[:, k, 3072:3104]) for k in range(KC)],
                      reads=(r_win, r_hT), wreg=self.psr[pg])
                sy.op("vector", lambda v: v.tensor_tensor(out=gpre[:, :, :].rearrange("p a h -> p (a h)"), in0=self.ps[pg][:, 0:32], in1=gb[:, :], op=ALU.add),
                      reads=(self.psr[pg], r_gb), writes=(r_gpre,))
                sy.op("scalar", lambda a: a.activation(out=lt[:, :, :], in_=gpre[:, 1::2, :], func=AF.Exp, scale=-1.0),
                      reads=(r_gpre,), writes=(r_lt,))
                sy.op("scalar", lambda a: a.activation(out=lt[:, :, :], in_=lt[:, :, :], func=AF.Ln, bias=1.0, scale=1.0),
                      reads=(r_lt,), writes=(r_lt,))
                yield
                sy.mm(self.ps[pg][:, 64:72], [(self.masks[:, 2, :], lt[:, 0, :])], reads=(r_lt, self.cst), wreg=self.psr[pg])
                sy.mm(self.ps[pg][:, 72:80], [(self.masks[:, 3, :], lt[:, 1, :])], reads=(r_lt, self.cst), wreg=self.psr[pg])
                sy.mm(self.ps[pg][:, 96:112], [(self.negones[:, :], lt[:, :, :].rearrange("p a h -> p (a h)"))], reads=(r_lt, self.cst), wreg=self.psr[pg])
                bps = self.ps[pg][:, 64:80].rearrange("p (a h) -> p a h", a=2)
                sy.op("vector", lambda v, bps=bps: v.tensor_tensor(out=dtmp[:, :, :], in0=gpre[:, 0::2, :], in1=bps, op=ALU.subtract),
                      reads=(r_gpre, self.psr[pg]), writes=(r_dt,))
                rg = self.R("gstat", c)
                sy.op("scalar", lambda a, c=c: a.activation(out=self.ws_all[:, c, :], in_=dtmp[:, :, :].rearrange("p a h -> p (a h)"), func=AF.Exp,
                                                          bias=float(np.log(0.125)), scale=1.0),
                      reads=(r_dt,), writes=(rg,))
                sy.op("scalar", lambda a, c=c: a.activation(out=self.thr_all[:, c, :], in_=self.ps[pg][:, 64:80], func=AF.Exp, scale=-1.0),
                      reads=(self.psr[pg],), writes=(rg,))
                sy.op("scalar", lambda a, c=c: a.activation(out=self.ebl_all[:, c, :], in_=self.ps[pg][:, 96:112], func=AF.Exp),
                      reads=(self.psr[pg],), writes=(rg,))

            run_pipeline([(lambda t4=t4: body(t4)) for t4 in range(w // 128)], depth=2, stagger=True)
        self.phase_end()

    def mlstm_conv(self):
        sy = self.sy
        self.phase_begin()
        W = 1024
        cw = self.tsb("c_cw", [128, KC, 5])
        u = [self.tsb(f"c_u{b}", [128, W + 4]) for b in range(2)]
        acc = [self.tsb(f"c_acc{b}", [128, W]) for b in range(2)]
        ob = [self.tsb(f"c_ob{b}", [128, W], BF16) for b in range(2)]
        r_cw = Reg()
        r_u, r_acc, r_ob = [Reg(), Reg()], [Reg(), Reg()], [Reg(), Reg()]
        r_uqk, r_q, r_k = self.R("uqkT"), self.R("qT"), self.R("kT")
        sy.dma("sync", cw[:, :, :], self.m_conv_w, writes=(r_cw,))
        blocks = [(0, 256, 0, 256)] + [(256 + s * W, W, 256, NT) for s in range(SEQ // W)]
        n = 0
        for oc in range(KC):
            for (c0, w, lo_lim, hi_lim) in blocks:
                b = n % 2
                n += 1
                hl = 2 if c0 - 2 >= lo_lim else 0
                hr = 2 if c0 + w + 2 <= hi_lim else 0
                if hl == 0:
                    sy.op("vector", lambda v, b=b: v.memset(u[b][:, 0:2], 0.0), writes=(r_u[b],))
                if hr == 0:
                    sy.op("vector", lambda v, b=b, w=w: v.memset(u[b][:, w + 2:w + 4], 0.0), writes=(r_u[b],))
                sy.dma("sync", u[b][:, 2 - hl:2 + w + hr], self.uqkT[oc * 128:(oc + 1) * 128, c0 - hl:c0 + w + hr],
                       reads=(r_uqk,), writes=(r_u[b],))
                sy.op("vector", lambda v, b=b, w=w, oc=oc: v.tensor_scalar(out=acc[b][:, 0:w], in0=u[b][:, 0:w], scalar1=cw[:, oc, 0:1], scalar2=None, op0=ALU.mult),
                      reads=(r_u[b], r_cw), writes=(r_acc[b],))
                for j in range(1, 5):
                    sy.op("vector", lambda v, b=b, w=w, oc=oc, j=j: v.scalar_tensor_tensor(
                        out=acc[b][:, 0:w], in0=u[b][:, j:j + w], scalar=cw[:, oc, j:j + 1], in1=acc[b][:, 0:w], op0=ALU.mult, op1=ALU.add),
                        reads=(r_u[b], r_cw, r_acc[b]), writes=(r_acc[b],))
                sy.op("scalar", lambda a, b=b, w=w: a.activation(out=ob[b][:, 0:w], in_=acc[b][:, 0:w], func=AF.Silu),
                      reads=(r_acc[b],), writes=(r_ob[b],))
                dst = self.qT if oc < 4 else self.kT
                sy.dma("gpsimd", dst[(oc % 4) * 128:(oc % 4 + 1) * 128, c0:c0 + w], ob[b][:, 0:w], reads=(r_ob[b],),
                       writes=(r_q if oc < 4 else r_k,))
        self.phase_end()

    def mlstm_scan(self, i):
        sy = self.sy
        self.phase_begin()
        qTc = [self.tsb(f"s_q{b}", [128, 4, 128], BF16) for b in range(4)]
        kTc = [self.tsb(f"s_k{b}", [128, 4, 128], BF16) for b in range(4)]
        vtc = [self.tsb(f"s_v{b}", [128, D], BF16) for b in range(4)]
        kTm = [self.tsb(f"s_kTm{b}", [128, 8, 128], BF16) for b in range(2)]
        kw = [self.tsb(f"s_kw{b}", [128, 512], BF16) for b in range(2)]
        Pp = [self.tsb(f"s_P{b}", [128, 8, 128], BF16) for b in range(2)]
        hout = [self.tsb(f"s_hout{b}", [128, 8, 128]) for b in range(2)]
        sm = [self.tsb(f"s_sm{b}", [128, 32]) for b in range(2)]
        C = [self.tsb(f"s_C{d}", [128, 4, 128]) for d in range(2)]
        Ctmp = [self.tsb(f"s_Ctmp{d}", [128, 4, 128]) for d in range(2)]
        Cbf = [self.tsb(f"s_Cbf{d}", [128, 8, 128], BF16) for d in range(2)]
        nst = [self.tsb(f"s_n{d}", [128, 4]) for d in range(2)]
        ntmp = [self.tsb(f"s_ntmp{d}", [128, 4]) for d in range(2)]
        nbf = [self.tsb(f"s_nbf{d}", [128, 8, 2], BF16) for d in range(2)]
        r_kTm, r_kw, r_P, r_hout, r_sm = ([Reg(), Reg()] for _ in range(5))
        r_q, r_k, r_v = ([Reg() for _ in range(4)] for _ in range(3))
        r_C, r_Ct, r_Cbf, r_n, r_nt, r_nbf = ([Reg(), Reg()] for _ in range(6))
        r_h = [self.R("hfw"), self.R("hbw")]
        hdst = [self.hfw, self.hbw]
        qTv = self.qT.rearrange("(i p) t -> p i t", p=128)
        kTv = self.kT.rearrange("(i p) t -> p i t", p=128)
        nctx = CTX // 128
        orders = [list(range(NCH)), list(range(nctx - 1, -1, -1)) + list(range(NCH - 1, nctx - 1, -1))]
        for b in range(2):
            sy.op("vector", lambda v, b=b: v.memset(kTm[b][:, :, :], 0.0), writes=(r_kTm[b],))
        for d in range(2):
            sy.op("vector", lambda v, d=d: v.memset(C[d][:, :, :], 0.0), writes=(r_C[d],))
            sy.op("vector", lambda v, d=d: v.memset(Cbf[d][:, :, :], 0.0), writes=(r_Cbf[d],))
            sy.op("vector", lambda v, d=d: v.memset(nst[d][:, :], 0.0), writes=(r_n[d],))
            sy.op("vector", lambda v, d=d: v.memset(nbf[d][:, :, :], 0.0), writes=(r_nbf[d],))
        n = 0
        def body(step, d):
            if True:
                c = orders[d][step]
                b = d
                psA = psD = 1 + 3 * d
                psS = psN = psC = (2 + 3 * d, 3 + 3 * d)
                DO = 256
                lb = 2 * d + step % 2
                tok = slice(c * 128, (c + 1) * 128)
                rg = self.R("gstat", c)
                ws = self.ws_all[:, c, d * 8:(d + 1) * 8]
                thr = self.thr_all[:, c, d * 8:(d + 1) * 8]
                ebl = self.ebl_all[:, c, d * 8:(d + 1) * 8]
                sy.dma("sync", qTc[lb][:, :, :], qTv[:, :, tok], reads=(self.R("qT"),), writes=(r_q[lb],))
                sy.dma("sync", kTc[lb][:, :, :], kTv[:, :, tok], reads=(self.R("kT"),), writes=(r_k[lb],))
                sy.dma("sync", vtc[lb][:, :], self.vtok[tok, :], reads=(self.R("vtok"),), writes=(r_v[lb],))
                yield
                pA = self.ps[psA][:, :].bitcast(BF16)
                for i4 in range(4):
                    sy.op("tensor", lambda t, lb=lb, i4=i4, b=b, pA=pA: t.transpose(pA[:, i4 * 128:(i4 + 1) * 128], kTc[lb][:, i4, :], self.ident_b[:, :]),
                          reads=(r_k[lb], self.R("identb")), writes=(self.psr[psA],) if i4 == 0 else (), inc=(i4 == 3))
                e = sy.eng["tensor"]
                self.psr[psA].w = {e["sem"]: e["cnt"]}
                sy.op("vector", lambda v, pA=pA, ws=ws, b=b: v.tensor_tensor(
                    out=kw[b][:, :].rearrange("p (h e) -> p h e", h=8), in0=pA[:, 0:512].rearrange("p (h e) -> p h e", h=8),
                    in1=ws.unsqueeze(2).to_broadcast([128, 8, 64]), op=ALU.mult),
                    reads=(self.psr[psA], rg), writes=(r_kw[b],))
                for hh in range(2):
                    rows = slice(hh * 64, hh * 64 + 64)
                    sy.op("vector", lambda g, lb=lb, rows=rows, hh=hh, b=b: g.tensor_copy(out=kTm[b][rows, hh::2, :], in_=kTc[lb][rows, :, :]),
                          reads=(r_k[lb],), writes=(r_kTm[b],))
                yield
                for h in range(8):
                    bank = psS[h // 4]
                    col = slice((h % 4) * 128, (h % 4 + 1) * 128)
                    sy.mm(self.ps[bank][:, col], [(kTm[b][:, h, :], qTc[lb][:, h // 2, :])], reads=(r_kTm[b], r_q[lb]),
                          wreg=self.psr[bank] if h % 4 == 0 else Reg())
                    if h % 4 == 3:
                        e = sy.eng["tensor"]
                        self.psr[bank].w = {e["sem"]: e["cnt"]}
                yield
                for h in range(8):
                    bank = psS[h // 4]
                    col = slice((h % 4) * 128, (h % 4 + 1) * 128)
                    sy.op("vector", lambda v, h=h, bank=bank, col=col, ws=ws, d=d, b=b: v.scalar_tensor_tensor(
                        out=Pp[b][:, h, :], in0=self.ps[bank][:, col], scalar=ws[:, h:h + 1], in1=self.masks[:, d, :], op0=ALU.mult, op1=ALU.mult),
                        reads=(self.psr[bank], rg, self.cst), writes=(r_P[b],))
                yield
                for h in range(8):
                    bank = psN[h // 4]
                    col = slice((h % 4) * 128, (h % 4 + 1) * 128)
                    sy.mm(self.ps[bank][:, col], [(qTc[lb][:, h // 2, :], Cbf[d][:, h, :]), (Pp[b][:, h, :], vtc[lb][:, h * 128:(h + 1) * 128])],
                          reads=(r_q[lb], r_Cbf[d], r_P[b], r_v[lb]), wreg=self.psr[bank] if h % 4 == 0 else Reg())
                    if h % 4 == 3:
                        e = sy.eng["tensor"]
                        self.psr[bank].w = {e["sem"]: e["cnt"]}
                for h in range(8):
                    sy.mm(self.ps[psD][:, DO + 2 * h:DO + 2 * h + 2], [(qTc[lb][:, h // 2, :], nbf[d][:, h, :]), (Pp[b][:, h, :], self.ones_col[:, :])],
                          reads=(r_q[lb], r_nbf[d], r_P[b], self.cst), wreg=self.psr[psD] if h == 0 else Reg())
                e = sy.eng["tensor"]
                self.psr[psD].w = {e["sem"]: e["cnt"]}
                yield
                smb = sm[b]
                sy.op("vector", lambda v, smb=smb: v.tensor_scalar(out=smb[:, 0:8], in0=self.ps[psD][:, DO:DO + 16:2], scalar1=-1.0, scalar2=None, op0=ALU.mult),
                      reads=(self.psr[psD],), writes=(r_sm[b],))
                sy.op("vector", lambda v, smb=smb: v.tensor_tensor(out=smb[:, 8:16], in0=self.ps[psD][:, DO:DO + 16:2], in1=smb[:, 0:8], op=ALU.max),
                      reads=(self.psr[psD], r_sm[b]), writes=(r_sm[b],))
                sy.op("vector", lambda v, thr=thr, smb=smb: v.tensor_tensor(out=smb[:, 16:24], in0=smb[:, 8:16], in1=thr, op=ALU.max),
                      reads=(r_sm[b], rg), writes=(r_sm[b],))
                sy.op("vector", lambda v, smb=smb: v.reciprocal(out=smb[:, 24:32], in_=smb[:, 16:24]), reads=(r_sm[b],), writes=(r_sm[b],))
                ho = hout[b]
                for hb_ in range(2):
                    sy.op("vector", lambda v, hb_=hb_, ho=ho, smb=smb: v.tensor_tensor(
                        out=ho[:, hb_ * 4:(hb_ + 1) * 4, :], in0=self.ps[psN[hb_]][:, :].rearrange("p (h e) -> p h e", h=4),
                        in1=smb[:, 24 + hb_ * 4:28 + hb_ * 4].unsqueeze(2).to_broadcast([128, 4, 128]), op=ALU.mult),
                        reads=(self.psr[psN[hb_]], r_sm[b]), writes=(r_hout[b],))
                yield
                for i4 in range(4):
                    for hh in range(2):
                        bank = psC[hh]
                        sy.mm(self.ps[bank][:, i4 * 128:(i4 + 1) * 128], [(kw[b][:, i4 * 128:(i4 + 1) * 128], vtc[lb][:, (2 * i4 + hh) * 128:(2 * i4 + hh + 1) * 128])],
                              reads=(r_kw[b], r_v[lb]), wreg=self.psr[bank] if i4 == 0 else Reg())
                for i4 in range(4):
                    sy.mm(self.ps[psD][:, DO + 16 + 2 * i4:DO + 18 + 2 * i4], [(kw[b][:, i4 * 128:(i4 + 1) * 128], self.ones_col[:, :])],
                          reads=(r_kw[b], self.cst), wreg=Reg())
                e = sy.eng["tensor"]
                for bk in (psD, psC[0], psC[1]):
                    self.psr[bk].w = {e["sem"]: e["cnt"]}
                yield
                for hh in range(2):
                    rows = slice(hh * 64, hh * 64 + 64)
                    esel = ebl[rows, hh::2]
                    sy.op("vector", lambda v, rows=rows, hh=hh, d=d: v.tensor_tensor(
                        out=Ctmp[d][rows, :, :], in0=self.ps[psC[hh]][rows, :].rearrange("p (i e) -> p i e", i=4), in1=C[d][rows, :, :], op=ALU.add),
                        reads=(self.psr[psC[hh]], r_C[d]), writes=(r_Ct[d],))
                    sy.op("vector", lambda v, rows=rows, esel=esel, d=d: v.tensor_tensor(
                        out=C[d][rows, :, :], in0=Ctmp[d][rows, :, :], in1=esel.unsqueeze(2).to_broadcast([64, 4, 128]), op=ALU.mult),
                        reads=(r_Ct[d], rg, r_C[d]), writes=(r_C[d],))
                    sy.op("vector", lambda v, rows=rows, d=d: v.tensor_tensor(
                        out=ntmp[d][rows, :], in0=self.ps[psD][rows, DO + 16:DO + 24:2], in1=nst[d][rows, :], op=ALU.add),
                        reads=(self.psr[psD], r_n[d]), writes=(r_nt[d],))
                    sy.op("vector", lambda v, rows=rows, esel=esel, d=d: v.tensor_tensor(
                        out=nst[d][rows, :], in0=ntmp[d][rows, :], in1=esel, op=ALU.mult),
                        reads=(r_nt[d], rg, r_n[d]), writes=(r_n[d],))
                for hh in range(2):
                    rows = slice(hh * 64, hh * 64 + 64)
                    sy.op("scalar", lambda a, rows=rows, hh=hh, d=d: a.copy(out=Cbf[d][rows, hh::2, :], in_=C[d][rows, :, :]), reads=(r_C[d],), writes=(r_Cbf[d],))
                    sy.op("scalar", lambda a, rows=rows, hh=hh, d=d: a.copy(out=nbf[d][rows, hh::2, :], in_=nst[d][rows, :].unsqueeze(2).to_broadcast([64, 4, 2])),
                          reads=(r_n[d],), writes=(r_nbf[d],))
                sy.dma("gpsimd", hdst[d][tok, :], ho[:, :, :].rearrange("p h e -> p (h e)"), reads=(r_hout[b],), writes=(r_h[d],))

        def lockstep(step):
            g0, g1 = body(step, 0), body(step, 1)
            a0 = a1 = True
            while a0 or a1:
                if a0:
                    try:
                        next(g0)
                    except StopIteration:
                        a0 = False
                if a1:
                    try:
                        next(g1)
                    except StopIteration:
                        a1 = False
            return
            yield
        for step in range(NCH):
            for _ in lockstep(step):
                pass
        self.phase_end()

    def mlstm_epilogue(self, i):
        sy = self.sy
        hg, rmod = self.m_hg, self.m_rmod
        self.phase_begin()
        wout = self.tsb("e_wout", [128, KC, D], BF16)
        ngb = self.tsb("e_ngb", [128, D])
        otc = [self.tsb(f"e_o{b}", [128, D]) for b in range(2)]
        hfc = [self.tsb(f"e_hf{b}", [128, 8, 128]) for b in range(2)]
        hbc = [self.tsb(f"e_hb{b}", [128, 8, 128]) for b in range(2)]
        xc = [self.tsb(f"e_x{b}", [128, KC, 128]) for b in range(2)]
        cen = [self.tsb(f"e_cen{b}", [128, 8, 128]) for b in range(2)]
        sq = self.tsb("e_sq", [128, 8, 128])
        ybf = [self.tsb(f"e_y{b}", [128, D], BF16) for b in range(2)]
        yT = [self.tsb(f"e_yT{b}", [128, D], BF16) for b in range(2)]
        sm = [self.tsb(f"e_sm{b}", [128, 32]) for b in range(2)]
        r_wout, r_ngb, r_sq = Reg(), Reg(), Reg()
        r_o, r_hf, r_hb, r_x, r_cen, r_y, r_yT, r_sm = ([Reg(), Reg()] for _ in range(8))
        r_xs = self.R("xT", id(self.xs))
        for k in range(KC):
            sy.dma("sync", wout[:, k, :], self.m_wouts[:, k, :], reads=(self.R("m_wouts"),), writes=(r_wout,))
        sy.dma("sync", ngb[:, :], self.m_norm_g, writes=(r_ngb,))
        xsv = self.xs.rearrange("(k p) t -> p k t", p=128)
        nctx = CTX // 128
        def body(c):
            b = c % 2
            tok = slice(c * 128, (c + 1) * 128)
            sy.dma("sync", otc[b][:, :], self.otok[tok, :], reads=(self.R("otok"),), writes=(r_o[b],))
            sy.dma("sync", hfc[b][:, :, :].rearrange("p h e -> p (h e)"), self.hfw[tok, :], reads=(self.R("hfw"),), writes=(r_hf[b],))
            sy.dma("sync", hbc[b][:, :, :].rearrange("p h e -> p (h e)"), self.hbw[tok, :], reads=(self.R("hbw"),), writes=(r_hb[b],))
            sy.dma("sync", xc[b][:, :, :], xsv[:, :, tok], reads=(r_xs,), writes=(r_x[b],))
            ho, smb, ce = hfc[b], sm[b], cen[b]
            yield
            sy.op("vector", lambda v, ho=ho, b=b: v.tensor_tensor(out=ho[:, :, :], in0=ho[:, :, :], in1=hbc[b][:, :, :], op=ALU.add),
                  reads=(r_hf[b], r_hb[b]), writes=(r_hf[b],))
            sy.op("vector", lambda v, ho=ho, smb=smb: v.tensor_reduce(out=smb[:, 0:8], in_=ho[:, :, :], axis=AX.X, op=ALU.add),
                  reads=(r_hf[b],), writes=(r_sm[b],))
            sy.op("vector", lambda v, smb=smb: v.tensor_scalar(out=smb[:, 8:16], in0=smb[:, 0:8], scalar1=1.0 / 128.0, scalar2=None, op0=ALU.mult),
                  reads=(r_sm[b],), writes=(r_sm[b],))
            sy.op("vector", lambda v, ho=ho, smb=smb, ce=ce: v.tensor_tensor(out=ce[:, :, :], in0=ho[:, :, :], in1=smb[:, 8:16].unsqueeze(2).to_broadcast([128, 8, 128]), op=ALU.subtract),
                  reads=(r_hf[b], r_sm[b]), writes=(r_cen[b],))
            yield
            sy.op("scalar", lambda a, ce=ce: a.activation(out=sq[:, :, :], in_=ce[:, :, :], func=AF.Square), reads=(r_cen[b],), writes=(r_sq,))
            sy.op("vector", lambda v, smb=smb: v.tensor_reduce(out=smb[:, 16:24], in_=sq[:, :, :], axis=AX.X, op=ALU.add),
                  reads=(r_sq,), writes=(r_sm[b],))
            sy.op("scalar", lambda a, smb=smb: a.activation(out=smb[:, 24:32], in_=smb[:, 16:24], func=AF.Ln, bias=EPS, scale=1.0 / 128.0),
                  reads=(r_sm[b],), writes=(r_sm[b],))
            sy.op("scalar", lambda a, smb=smb: a.activation(out=smb[:, 24:32], in_=smb[:, 24:32], func=AF.Exp, scale=-0.5),
                  reads=(r_sm[b],), writes=(r_sm[b],))
            yield
            sy.op("scalar", lambda a, b=b: a.activation(out=otc[b][:, :], in_=otc[b][:, :], func=AF.Sigmoid),
                  reads=(r_o[b],), writes=(r_o[b],))
            sy.op("vector", lambda v, smb=smb, ce=ce: v.tensor_tensor(out=ce[:, :, :], in0=ce[:, :, :], in1=smb[:, 24:32].unsqueeze(2).to_broadcast([128, 8, 128]), op=ALU.mult),
                  reads=(r_cen[b], r_sm[b]), writes=(r_cen[b],))
            cen2 = ce[:, :, :].rearrange("p h e -> p (h e)")
            sy.op("vector", lambda v, cen2=cen2: v.tensor_tensor(out=cen2, in0=cen2, in1=ngb[:, :], op=ALU.mult),
                  reads=(r_cen[b], r_ngb), writes=(r_cen[b],))
            sy.op("vector", lambda v, cen2=cen2, b=b: v.tensor_tensor(out=ybf[b][:, :], in0=cen2, in1=otc[b][:, :], op=ALU.mult),
                  reads=(r_cen[b], r_o[b]), writes=(r_y[b],))
            yield
            pbank = b
            pT = self.ps[pbank][:, :].bitcast(BF16)
            for k in range(KC):
                sy.op("tensor", lambda t, k=k, pT=pT, b=b: t.transpose(pT[:, k * 128:(k + 1) * 128], ybf[b][:, k * 128:(k + 1) * 128], self.ident_b[:, :]),
                      reads=(r_y[b], self.R("identb")), writes=(self.psr[pbank],) if k == 0 else (), inc=(k == KC - 1))
            e = sy.eng["tensor"]
            self.psr[pbank].w = {e["sem"]: e["cnt"]}
            sy.op("scalar", lambda a, pT=pT, b=b: a.copy(out=yT[b][:, :], in_=pT[:, :]), reads=(self.psr[pbank],), writes=(r_yT[b],))
            yield
            for oc in range(KC):
                bank = 2 + 2 * b + oc // 4
                col = slice((oc % 4) * 128, (oc % 4 + 1) * 128)
                sy.mm(self.ps[bank][:, col], [(wout[:, k, oc * 128:(oc + 1) * 128], yT[b][:, k * 128:(k + 1) * 128]) for k in range(KC)],
                      reads=(r_wout, r_yT[b]), wreg=self.psr[bank] if oc % 4 == 0 else Reg())
                if oc % 4 == 3:
                    e = sy.eng["tensor"]
                    self.psr[bank].w = {e["sem"]: e["cnt"]}
            yield
            v_ = 0 if c >= nctx else 1
            for oh_ in range(2):
                bank = 2 + 2 * b + oh_
                sy.op("vector", lambda v, bank=bank, oh_=oh_, v_=v_, b=b: v.tensor_tensor(
                    out=cen[b][:, oh_ * 4:(oh_ + 1) * 4, :], in0=self.ps[bank][:, :].rearrange("p (o t) -> p o t", o=4),
                    in1=hg[:, v_, oh_ * 4:(oh_ + 1) * 4].unsqueeze(2).to_broadcast([128, 4, 128]), op=ALU.mult),
                    reads=(self.psr[bank], rmod, r_cen[b]), writes=(r_cen[b],))
            sy.op("vector", lambda v, b=b: v.tensor_tensor(out=xc[b][:, :, :], in0=xc[b][:, :, :], in1=cen[b][:, :, :], op=ALU.add),
                  reads=(r_x[b], r_cen[b]), writes=(r_x[b],))
            sy.dma("gpsimd", xsv[:, :, tok], xc[b][:, :, :], reads=(r_x[b],), writes=(r_xs,))

        run_pipeline([(lambda c=c: body(c)) for c in range(NCH)], depth=2)
        self.phase_end()

    def precast_ret(self):
        sy = self.sy
        r = self.R("r_wins")
        for k in range(KC):
            sy.dma("gpsimd", self.r_wins[:, k, :], self.r_w_in[k * 128:(k + 1) * 128, :], writes=(r,))
        r2 = self.R("r_wouts")
        for k in range(16):
            sy.dma("gpsimd", self.r_wouts[:, k, :], self.r_w_out[k * 128:(k + 1) * 128, :], writes=(r2,))

    def ret_inproj(self, i):
        sy = self.sy
        gs, sh, hg, rmod = self.sub_mod(i, 1, 1.0)
        self.r_hg = hg
        self.r_rmod = rmod
        self.phase_begin()
        win = self.tsb("r_win", [128, KC, R_IN], BF16)
        xt = self.tsb("r_xt", [128, KC, 512])
        hT = self.tsb("r_hT", [128, KC, 512], BF16)
        qk = self.tsb("r_qk", [128, 2, D])
        cs = self.tsb("r_cs", [128, 2, 128])
        t1 = self.tsb("r_t1", [128, 4, 128])
        t2 = self.tsb("r_t2", [128, 4, 128])
        qkr = self.tsb("r_qkr", [128, 2, D], BF16)
        qkT = self.tsb("r_qkT", [128, 2, D], BF16)
        stv = self.tsb("r_stv", [128, 2048], BF16)
        stg = self.tsb("r_stg", [128, 2048])
        r_win, r_xt, r_hT, r_qk, r_cs, r_t1, r_t2, r_qkr, r_qkT, r_stv, r_stg = (Reg() for _ in range(11))
        r_xs = self.R("xT", id(self.xs))
        for k in range(KC):
            sy.dma("sync", win[:, k, :], self.r_wins[:, k, :], reads=(self.R("r_wins"),), writes=(r_win,))
        for (c0, w, kind) in self.pieces():
            for k in range(KC):
                sy.dma("sync", xt[:, k, 0:w], self.xs[k * 128:(k + 1) * 128, c0:c0 + w], reads=(r_xs,), writes=(r_xt,))
            self.modnorm_piece(xt, r_xt, 0, w, kind, gs, sh, rmod, hT, r_hT, 0)
            def body(t4, c0=c0, w=w):
                c = c0 // 128 + t4
                tsl = slice(t4 * 128, (t4 + 1) * 128)
                tok = slice(c * 128, (c + 1) * 128)
                sy.dma("sync", cs[:, :, :], self.rope_cs[tok, :, :], writes=(r_cs,))
                for n2 in range(4):
                    pb = n2 % 4
                    sy.mm(self.ps[pb][:, :], [(hT[:, k, tsl], win[:, k, n2 * 512:(n2 + 1) * 512]) for k in range(KC)],
                          reads=(r_win, r_hT), wreg=self.psr[pb])
                    if n2 < 4:
                        sy.op("scalar", lambda a, pb=pb, n2=n2: a.copy(out=qk[:, n2 // 2, (n2 % 2) * 512:(n2 % 2 + 1) * 512], in_=self.ps[pb][:, :]),
                              reads=(self.psr[pb],), writes=(r_qk,))
                    elif n2 < 8:
                        sy.op("vector", lambda v, pb=pb, n2=n2: v.tensor_copy(out=stv[:, (n2 - 4) * 512:(n2 - 3) * 512], in_=self.ps[pb][:, :]),
                              reads=(self.psr[pb],), writes=(r_stv,))
                    else:
                        sy.op("scalar", lambda a, pb=pb, n2=n2: a.copy(out=stg[:, (n2 - 8) * 512:(n2 - 7) * 512], in_=self.ps[pb][:, :]),
                              reads=(self.psr[pb],), writes=(r_stg,))
                yield
                cosb = cs[:, 0, :].unsqueeze(1).to_broadcast([128, 4, 128])
                sinb = cs[:, 1, :].unsqueeze(1).to_broadcast([128, 4, 128])
                for a_ in range(2):
                    src = qk[:, a_, :].rearrange("p (h j t) -> p h j t", h=4, t=2)
                    dst = qkr[:, a_, :].rearrange("p (h j t) -> p h j t", h=4, t=2)
                    e_, o_ = src[:, :, :, 0], src[:, :, :, 1]
                    sy.op("vector", lambda v, e_=e_: v.tensor_tensor(out=t1[:, :, :], in0=e_, in1=cosb, op=ALU.mult), reads=(r_qk, r_cs), writes=(r_t1,))
                    sy.op("vector", lambda v, o_=o_: v.tensor_tensor(out=t2[:, :, :], in0=o_, in1=sinb, op=ALU.mult), reads=(r_qk, r_cs), writes=(r_t2,))
                    sy.op("vector", lambda v, dst=dst: v.tensor_tensor(out=dst[:, :, :, 0], in0=t1[:, :, :], in1=t2[:, :, :], op=ALU.subtract), reads=(r_t1, r_t2), writes=(r_qkr,))
                    sy.op("vector", lambda v, e_=e_: v.tensor_tensor(out=t1[:, :, :], in0=e_, in1=sinb, op=ALU.mult), reads=(r_qk, r_cs), writes=(r_t1,))
                    sy.op("vector", lambda v, o_=o_: v.tensor_tensor(out=t2[:, :, :], in0=o_, in1=cosb, op=ALU.mult), reads=(r_qk, r_cs), writes=(r_t2,))
                    sy.op("vector", lambda v, dst=dst: v.tensor_tensor(out=dst[:, :, :, 1], in0=t1[:, :, :], in1=t2[:, :, :], op=ALU.add), reads=(r_t1, r_t2), writes=(r_qkr,))
                yield
                sy.dma("gpsimd", self.rktok[tok, :], qkr[:, 1, :], reads=(r_qkr,), writes=(self.R("rktok"),))
                for a_ in range(2):
                    pT = self.ps[4 + a_][:, :].bitcast(BF16)
                    for j in range(8):
                        sy.op("tensor", lambda t, j=j, a_=a_, pT=pT: t.transpose(pT[:, j * 128:(j + 1) * 128], qkr[:, a_, j * 128:(j + 1) * 128], self.ident_b[:, :]),
                              reads=(r_qkr, self.R("identb")), writes=(self.psr[4 + a_],) if j == 0 else (), inc=(j == 7))
                    e = sy.eng["tensor"]
                    self.psr[4 + a_].w = {e["sem"]: e["cnt"]}
                    sy.op("scalar", lambda a, a_=a_, pT=pT: a.copy(out=qkT[:, a_, :], in_=pT[:, :]), reads=(self.psr[4 + a_],), writes=(r_qkT,))
                sy.dma("gpsimd", self.rqT[c], qkT[:, 0, :], reads=(r_qkT,), writes=(self.R("rqT"),))
                sy.dma("gpsimd", self.rkT[c], qkT[:, 1, :], reads=(r_qkT,), writes=(self.R("rkT"),))
                yield
                for n2 in range(4, 8):
                    pb = n2 % 4
                    sy.mm(self.ps[pb][:, :], [(hT[:, k, tsl], win[:, k, n2 * 512:(n2 + 1) * 512]) for k in range(KC)],
                          reads=(r_win, r_hT), wreg=self.psr[pb])
                    sy.op("vector", lambda v, pb=pb, n2=n2: v.tensor_copy(out=stv[:, (n2 - 4) * 512:(n2 - 3) * 512], in_=self.ps[pb][:, :]),
                          reads=(self.psr[pb],), writes=(r_stv,))
                sy.dma("gpsimd", self.rvtok[tok, :], stv[:, :], reads=(r_stv,), writes=(self.R("rvtok"),))
                yield
                for n2 in range(8, 12):
                    pb = n2 % 4
                    sy.mm(self.ps[pb][:, :], [(hT[:, k, tsl], win[:, k, n2 * 512:(n2 + 1) * 512]) for k in range(KC)],
                          reads=(r_win, r_hT), wreg=self.psr[pb])
                    sy.op("scalar", lambda a, pb=pb, n2=n2: a.copy(out=stg[:, (n2 - 8) * 512:(n2 - 7) * 512], in_=self.ps[pb][:, :]),
                          reads=(self.psr[pb],), writes=(r_stg,))
                sy.dma("gpsimd", self.rgtok[tok, :], stg[:, :], reads=(r_stg,), writes=(self.R("rgtok"),))

            run_pipeline([(lambda t4=t4: body(t4)) for t4 in range(w // 128)], depth=2, stagger=True)
        self.phase_end()

    def ret_consts(self, tsb_prefix):
        sy = self.sy
        dec = self.tsb(tsb_prefix + "dec", [128, 8])
        lgn = self.tsb(tsb_prefix + "lgn", [128, 8])
        pos = self.tsb(tsb_prefix + "pos", [128, 4])
        xi = self.tsb(tsb_prefix + "xi", [128, 8])
        kap = self.tsb(tsb_prefix + "kap", [128, 8])
        kapg = self.tsb(tsb_prefix + "kapg", [128, 8])
        g128 = self.tsb(tsb_prefix + "g128", [128, 8])
        kmask = self.tsb(tsb_prefix + "kmask", [128, 2, 4, 128])
        r_cst = Reg()
        sy.dma("sync", dec[:, :], self.r_decay, writes=(r_cst,))
        sy.dma("sync", pos[:, :], self.r_pos, writes=(r_cst,))
        sy.op("scalar", lambda a: a.activation(out=lgn[:, :], in_=dec[:, :], func=AF.Exp, scale=-1.0), reads=(r_cst,), writes=(r_cst,))
        sy.op("scalar", lambda a: a.activation(out=lgn[:, :], in_=lgn[:, :], func=AF.Ln, bias=1.0, scale=1.0), reads=(r_cst,), writes=(r_cst,))
        for d in range(2):
            for h in range(4):
                j = d * 4 + h
                sy.op("scalar", lambda a, j=j, d=d: a.activation(out=xi[:, j:j + 1], in_=pos[:, d:d + 1], func=AF.Exp, scale=lgn[:, j:j + 1]),
                      reads=(r_cst,), writes=(r_cst,))
                sy.op("scalar", lambda a, j=j, d=d: a.activation(out=kap[:, j:j + 1], in_=pos[:, 2 + d:3 + d], func=AF.Exp, scale=lgn[:, j:j + 1],
                                                               bias=float(np.log(1.0 / 16.0))),
                      reads=(r_cst,), writes=(r_cst,))
        sy.op("scalar", lambda a: a.activation(out=g128[:, :], in_=lgn[:, :], func=AF.Exp, scale=-128.0), reads=(r_cst,), writes=(r_cst,))
        sy.op("vector", lambda v: v.tensor_tensor(out=kapg[:, :], in0=kap[:, :], in1=g128[:, :], op=ALU.mult), reads=(r_cst,), writes=(r_cst,))
        for d in range(2):
            for h in range(4):
                j = d * 4 + h
                sy.op("vector", lambda v, d=d, h=h, j=j: v.tensor_scalar(out=kmask[:, d, h, :], in0=self.masks[:, d, :], scalar1=kap[:, j:j + 1], scalar2=None, op0=ALU.mult),
                      reads=(r_cst, self.cst), writes=(r_cst,))
        return xi, kapg, g128, kmask, r_cst

    def ret_scan(self, i):
        sy = self.sy
        self.phase_begin()
        xi, kapg, g128, kmask, r_cst = self.ret_consts("t_")
        qTc = [self.tsb(f"t_q{b}", [128, 8, 128], BF16) for b in range(4)]
        kTc = [self.tsb(f"t_k{b}", [128, 8, 128], BF16) for b in range(4)]
        ktc = [self.tsb(f"t_kt{b}", [128, D], BF16) for b in range(4)]
        vtc = [self.tsb(f"t_v{b}", [128, 2048], BF16) for b in range(4)]
        kh = [self.tsb(f"t_kh{b}", [128, D], BF16) for b in range(2)]
        Pp = [self.tsb(f"t_P{b}", [128, 4, 128], BF16) for b in range(2)]
        oh = [self.tsb(f"t_oh{b}", [128, 4, 512]) for b in range(2)]
        Rs = [self.tsb(f"t_R{d}", [128, 8, 512]) for d in range(2)]
        Rbf = [self.tsb(f"t_Rbf{d}", [128, 8, 512], BF16) for d in range(2)]
        r_kh, r_P, r_oh, r_R, r_Rbf = ([Reg(), Reg()] for _ in range(5))
        r_q, r_k, r_kt, r_v = ([Reg() for _ in range(4)] for _ in range(4))
        r_o = [self.R("rofw"), self.R("robw")]
        odst = [self.rofw, self.robw]
        nctx = CTX // 128
        orders = [list(range(NCH)), list(range(nctx - 1, -1, -1)) + list(range(NCH - 1, nctx - 1, -1))]
        for d in range(2):
            sy.op("vector", lambda v, d=d: v.memset(Rs[d][:, :, :], 0.0), writes=(r_R[d],))
            sy.op("vector", lambda v, d=d: v.memset(Rbf[d][:, :, :], 0.0), writes=(r_Rbf[d],))
        def body(step, d):
            if True:
                c = orders[d][step]
                b = d
                pS, pO, pR = 4 * d, 4 * d + 1, (4 * d + 2, 4 * d + 3)
                lb = 2 * d + step % 2
                tok = slice(c * 128, (c + 1) * 128)
                sy.dma("sync", qTc[lb][:, :, :], self.rqT[c].rearrange("p (j t) -> p j t", j=8), reads=(self.R("rqT"),), writes=(r_q[lb],))
                sy.dma("sync", kTc[lb][:, :, :], self.rkT[c].rearrange("p (j t) -> p j t", j=8), reads=(self.R("rkT"),), writes=(r_k[lb],))
                sy.dma("sync", ktc[lb][:, :], self.rktok[tok, :], reads=(self.R("rktok"),), writes=(r_kt[lb],))
                sy.dma("sync", vtc[lb][:, :], self.rvtok[tok, :], reads=(self.R("rvtok"),), writes=(r_v[lb],))
                yield
                sy.op("vector", lambda v, b=b, d=d: v.tensor_tensor(
                    out=kh[b][:, :].rearrange("p (h e) -> p h e", h=4), in0=ktc[lb][:, :].rearrange("p (h e) -> p h e", h=4),
                    in1=kapg[:, d * 4:(d + 1) * 4].unsqueeze(2).to_broadcast([128, 4, 256]), op=ALU.mult),
                    reads=(r_kt[lb], r_cst), writes=(r_kh[b],))
                if c >= nctx:
                    for h in range(4):
                        sy.mm(self.ps[pS][:, h * 128:(h + 1) * 128], [(kTc[lb][:, 2 * h + dc, :], qTc[lb][:, 2 * h + dc, :]) for dc in range(2)],
                              reads=(r_k[lb], r_q[lb]), wreg=self.psr[pS] if h == 0 else Reg())
                    e = sy.eng["tensor"]
                    self.psr[pS].w = {e["sem"]: e["cnt"]}
                    yield
                    sy.op("vector", lambda v, d=d, b=b: v.tensor_tensor(out=Pp[b][:, :, :], in0=self.ps[pS][:, :].rearrange("p (h t) -> p h t", h=4), in1=kmask[:, d, :, :], op=ALU.mult),
                          reads=(self.psr[pS], r_cst), writes=(r_P[b],))
                    yield
                    for h in range(4):
                        bank = pO
                        sy.mm(self.ps[bank][:, :], [(qTc[lb][:, 2 * h + dc, :], Rbf[d][:, 2 * h + dc, :]) for dc in range(2)] + [(Pp[b][:, h, :], vtc[lb][:, h * 512:(h + 1) * 512])],
                              reads=(r_q[lb], r_Rbf[d], r_P[b], r_v[lb]), wreg=self.psr[bank])
                        sy.op("scalar", lambda a, h=h, bank=bank, d=d, b=b: a.activation(out=oh[b][:, h, :], in_=self.ps[bank][:, :], func=AF.Copy, scale=xi[:, d * 4 + h:d * 4 + h + 1]),
                              reads=(self.psr[bank], r_cst), writes=(r_oh[b],))
                    sy.dma("gpsimd", odst[d][tok, :], oh[b][:, :, :].rearrange("p h e -> p (h e)"), reads=(r_oh[b],), writes=(r_o[d],))
                for h in range(4):
                    yield
                    for dc in range(2):
                        j = 2 * h + dc
                        bank = pR[j % 2]
                        sy.mm(self.ps[bank][:, :], [(kh[b][:, j * 128:(j + 1) * 128], vtc[lb][:, h * 512:(h + 1) * 512])], reads=(r_kh[b], r_v[lb]), wreg=self.psr[bank])
                        sy.op("vector", lambda v, j=j, h=h, d=d, bank=bank: v.scalar_tensor_tensor(
                            out=Rs[d][:, j, :], in0=Rs[d][:, j, :], scalar=g128[:, d * 4 + h:d * 4 + h + 1], in1=self.ps[bank][:, :], op0=ALU.mult, op1=ALU.add),
                            reads=(self.psr[bank], r_cst, r_R[d]), writes=(r_R[d],))
                        sy.op("scalar", lambda a, j=j, d=d: a.copy(out=Rbf[d][:, j, :], in_=Rs[d][:, j, :]), reads=(r_R[d],), writes=(r_Rbf[d],))

        for step in range(NCH):
            g0, g1 = body(step, 0), body(step, 1)
            a0 = a1 = True
            while a0 or a1:
                if a0:
                    try:
                        next(g0)
                    except StopIteration:
                        a0 = False
                if a1:
                    try:
                        next(g1)
                    except StopIteration:
                        a1 = False
        self.phase_end()

    def ret_epilogue(self, i):
        sy = self.sy
        hg, rmod = self.r_hg, self.r_rmod
        self.phase_begin()
        wout = self.tsb("u_wout", [128, 16, D], BF16)
        ngb = self.tsb("u_ngb", [128, 2048])
        gtc = [self.tsb(f"u_g{b}", [128, 2048]) for b in range(2)]
        ofc = [self.tsb(f"u_of{b}", [128, 4, 512]) for b in range(2)]
        obc = [self.tsb(f"u_ob{b}", [128, 4, 512]) for b in range(2)]
        xc = [self.tsb(f"u_x{b}", [128, KC, 128]) for b in range(2)]
        sq = self.tsb("u_sq", [128, 4, 512])
        ybf = [self.tsb(f"u_y{b}", [128, 2048], BF16) for b in range(2)]
        yT = [self.tsb(f"u_yT{b}", [128, 2048], BF16) for b in range(2)]
        sm = [self.tsb(f"u_sm{b}", [128, 32]) for b in range(2)]
        upd = self.tsb("u_upd", [128, KC, 128])
        r_wout, r_ngb, r_sq, r_upd = (Reg() for _ in range(4))
        r_g, r_of, r_ob, r_x, r_y, r_yT, r_sm = ([Reg(), Reg()] for _ in range(7))
        r_xs = self.R("xT", id(self.xs))
        for k in range(16):
            sy.dma("sync", wout[:, k, :], self.r_wouts[:, k, :], reads=(self.R("r_wouts"),), writes=(r_wout,))
        sy.dma("sync", ngb[:, :], self.r_norm_g, writes=(r_ngb,))
        xsv = self.xs.rearrange("(k p) t -> p k t", p=128)
        nctx = CTX // 128
        def body(c):
            b = c % 2
            tok = slice(c * 128, (c + 1) * 128)
            sy.dma("sync", gtc[b][:, :], self.rgtok[tok, :], reads=(self.R("rgtok"),), writes=(r_g[b],))
            sy.dma("sync", ofc[b][:, :, :].rearrange("p h e -> p (h e)"), self.rofw[tok, :], reads=(self.R("rofw"),), writes=(r_of[b],))
            sy.dma("sync", obc[b][:, :, :].rearrange("p h e -> p (h e)"), self.robw[tok, :], reads=(self.R("robw"),), writes=(r_ob[b],))
            sy.dma("sync", xc[b][:, :, :], xsv[:, :, tok], reads=(r_xs,), writes=(r_x[b],))
            o_, smb = ofc[b], sm[b]
            cen = obc[b]
            r_cen = r_ob[b]
            yield
            sy.op("vector", lambda v, o_=o_, b=b: v.tensor_tensor(out=o_[:, :, :], in0=o_[:, :, :], in1=obc[b][:, :, :], op=ALU.add), reads=(r_of[b], r_ob[b]), writes=(r_of[b],))
            sy.op("vector", lambda v, o_=o_, smb=smb: v.tensor_reduce(out=smb[:, 0:4], in_=o_[:, :, :], axis=AX.X, op=ALU.add), reads=(r_of[b],), writes=(r_sm[b],))
            sy.op("vector", lambda v, smb=smb: v.tensor_scalar(out=smb[:, 4:8], in0=smb[:, 0:4], scalar1=1.0 / 512.0, scalar2=None, op0=ALU.mult), reads=(r_sm[b],), writes=(r_sm[b],))
            sy.op("vector", lambda v, o_=o_, smb=smb: v.tensor_tensor(out=cen[:, :, :], in0=o_[:, :, :], in1=smb[:, 4:8].unsqueeze(2).to_broadcast([128, 4, 512]), op=ALU.subtract),
                  reads=(r_of[b], r_sm[b]), writes=(r_cen,))
            yield
            sy.op("scalar", lambda a: a.activation(out=sq[:, :, :], in_=cen[:, :, :], func=AF.Square), reads=(r_cen,), writes=(r_sq,))
            sy.op("vector", lambda v, smb=smb: v.tensor_reduce(out=smb[:, 8:12], in_=sq[:, :, :], axis=AX.X, op=ALU.add), reads=(r_sq,), writes=(r_sm[b],))
            sy.op("scalar", lambda a, smb=smb: a.activation(out=smb[:, 12:16], in_=smb[:, 8:12], func=AF.Ln, bias=EPS, scale=1.0 / 512.0), reads=(r_sm[b],), writes=(r_sm[b],))
            sy.op("scalar", lambda a, smb=smb: a.activation(out=smb[:, 12:16], in_=smb[:, 12:16], func=AF.Exp, scale=-0.5), reads=(r_sm[b],), writes=(r_sm[b],))
            yield
            sy.op("scalar", lambda a, b=b: a.activation(out=gtc[b][:, :], in_=gtc[b][:, :], func=AF.Silu), reads=(r_g[b],), writes=(r_g[b],))
            sy.op("vector", lambda v, smb=smb: v.tensor_tensor(out=cen[:, :, :], in0=cen[:, :, :], in1=smb[:, 12:16].unsqueeze(2).to_broadcast([128, 4, 512]), op=ALU.mult),
                  reads=(r_cen, r_sm[b]), writes=(r_cen,))
            cen2 = cen[:, :, :].rearrange("p h e -> p (h e)")
            sy.op("vector", lambda v, cen2=cen2: v.tensor_tensor(out=cen2, in0=cen2, in1=ngb[:, :], op=ALU.mult), reads=(r_cen, r_ngb), writes=(r_cen,))
            sy.op("vector", lambda v, cen2=cen2, b=b: v.tensor_tensor(out=ybf[b][:, :], in0=cen2, in1=gtc[b][:, :], op=ALU.mult), reads=(r_cen, r_g[b]), writes=(r_y[b],))
            yield
            for half in range(2):
                pbank = 2 * b + half
                pT = self.ps[pbank][:, :].bitcast(BF16)
                for k in range(8):
                    kk = half * 8 + k
                    sy.op("tensor", lambda t, k=k, kk=kk, pT=pT, b=b: t.transpose(pT[:, k * 128:(k + 1) * 128], ybf[b][:, kk * 128:(kk + 1) * 128], self.ident_b[:, :]),
                          reads=(r_y[b], self.R("identb")), writes=(self.psr[pbank],) if k == 0 else (), inc=(k == 7))
                e = sy.eng["tensor"]
                self.psr[pbank].w = {e["sem"]: e["cnt"]}
                sy.op("scalar", lambda a, half=half, pT=pT, b=b: a.copy(out=yT[b][:, half * 1024:(half + 1) * 1024], in_=pT[:, :]), reads=(self.psr[pbank],), writes=(r_yT[b],))
            yield
            for oc in range(KC):
                bank = 4 + 2 * b + oc // 4
                col = slice((oc % 4) * 128, (oc % 4 + 1) * 128)
                sy.mm(self.ps[bank][:, col], [(wout[:, k, oc * 128:(oc + 1) * 128], yT[b][:, k * 128:(k + 1) * 128]) for k in range(16)],
                      reads=(r_wout, r_yT[b]), wreg=self.psr[bank] if oc % 4 == 0 else Reg())
                if oc % 4 == 3:
                    e = sy.eng["tensor"]
                    self.psr[bank].w = {e["sem"]: e["cnt"]}
            yield
            for oh_ in range(2):
                bank = 4 + 2 * b + oh_
                sy.op("vector", lambda v, bank=bank, oh_=oh_: v.tensor_tensor(
                    out=upd[:, oh_ * 4:(oh_ + 1) * 4, :], in0=self.ps[bank][:, :].rearrange("p (o t) -> p o t", o=4),
                    in1=hg[:, 0, oh_ * 4:(oh_ + 1) * 4].unsqueeze(2).to_broadcast([128, 4, 128]), op=ALU.mult),
                    reads=(self.psr[bank], rmod), writes=(r_upd,))
            sy.op("vector", lambda v, b=b: v.tensor_tensor(out=xc[b][:, :, :], in0=xc[b][:, :, :], in1=upd[:, :, :], op=ALU.add),
                  reads=(r_x[b], r_upd), writes=(r_x[b],))
            sy.dma("gpsimd", xsv[:, :, tok], xc[b][:, :, :], reads=(r_x[b],), writes=(r_xs,))

        run_pipeline([(lambda c=c: body(c)) for c in range(nctx, NCH)], depth=2)
        self.phase_end()

    def dump_xs(self):
        dbg = self.nc.dram_tensor("dbgx", [D, NT], F32, kind="ExternalOutput").ap()
        r = self.R("dbgx")
        self.phase_begin()
        t = self.tsb("dump_t", [128, 2048])
        rt = Reg()
        for k in range(KC):
            for c0 in range(0, NT, 2048):
                w = min(2048, NT - c0)
                self.sy.dma("sync", t[:, 0:w], self.xs[k * 128:(k + 1) * 128, c0:c0 + w], reads=(self.R("xT", id(self.xs)),), writes=(rt,))
                self.sy.dma("sync", dbg[k * 128:(k + 1) * 128, c0:c0 + w], t[:, 0:w], reads=(rt,), writes=(r,))
        self.phase_end()
        self.sy.finish([r])

    def build(self):
        for (i, j) in ((0, 0), (0, 1), (1, 0), (1, 1)):
            if self.stop_after != "mod":
                self.precast_ffn(i, j)
        if self.stop_after == "mod":
            self.compute_mod((0, 1))
        else:
            self.compute_mod((0,))
        if self.stop_after == "mod":
            dbg = self.nc.dram_tensor("dbg", [128, 4, NMOD * KC], F32, kind="ExternalOutput").ap()
            r = self.R("dbg")
            self.sy.dma("sync", dbg[:, 0:2, :], self.modx[:, :, :], reads=(self.r_mod,), writes=(r,))
            self.sy.dma("sync", dbg[:, 2:4, :], self.modc[:, :, :], reads=(self.r_mod,), writes=(r,))
            self.sy.finish([r])
            return self.nc
        if self.stop_after == "ffn00":
            self.ffn(0, 0, self.xT_in, final=True)
            self.sy.finish([self.R("outT")])
            return self.nc
        self.precast_mlstm()
        self.ffn(0, 0, self.xT_in)
        self.mlstm_inproj(0)
        if self.stop_after == "minproj":
            self.dump_xs()
            return self.nc
        self.compute_mod((1,))
        self.mlstm_conv()
        if self.stop_after == "mconv":
            self.dump_xs()
            return self.nc
        self.mlstm_scan(0)
        if self.stop_after == "mscan":
            self.dump_xs()
            return self.nc
        self.mlstm_epilogue(0)
        if self.stop_after == "mlstm":
            self.dump_xs()
            return self.nc
        self.precast_ret()
        self.ffn(0, 1, self.xs)
        self.ffn(1, 0, self.xs)
        if self.stop_after == "ffn10":
            self.dump_xs()
            return self.nc
        self.ret_inproj(1)
        if self.stop_after == "rinproj":
            self.dump_xs()
            return self.nc
        self.ret_scan(1)
        if self.stop_after == "rscan":
            self.dump_xs()
            return self.nc
        self.ret_epilogue(1)
        if self.stop_after == "ret":
            self.dump_xs()
            return self.nc
        self.ffn(1, 1, self.xs, final=True)
        self.sy.finish([self.R("outT")])
        return self.nc


def host_layout(inp, b):
    f32 = np.float32
    x = np.asarray(inp["x"][b], f32)
    ctx = np.asarray(inp["ctx"][b], f32)
    xT = np.ascontiguousarray(np.concatenate([ctx, x], axis=0).T)
    cc = np.stack([np.asarray(inp["c"][b], f32), np.asarray(inp["c_ctx"], f32)], axis=-1)
    cT = np.ascontiguousarray(cc.reshape(KC, 128, 2).transpose(1, 0, 2))
    mod_b = np.ascontiguousarray(np.asarray(inp["mod_b"], f32).reshape(2, NMOD * KC, 128).transpose(0, 2, 1))
    norm_g = np.ascontiguousarray(np.asarray(inp["norm_g"], f32).reshape(2 * 3 * KC, 128).T)
    final_g = np.ascontiguousarray(np.asarray(inp["final_g"], f32).reshape(KC, 128).T)
    m = {
        "xT": xT, "cT": cT, "mod_w": np.asarray(inp["mod_w"], f32), "mod_b": mod_b, "norm_g": norm_g,
        "ffn_w13": np.asarray(inp["ffn_w13"], f32), "ffn_w2": np.asarray(inp["ffn_w2"], f32),
        "final_g": final_g, "ident": np.eye(128, dtype=f32),
        "m_w_in": np.asarray(inp["m_w_in"][0], f32), "m_w_out": np.asarray(inp["m_w_out"][0], f32),
        "m_gate_b": np.ascontiguousarray(np.broadcast_to(np.asarray(inp["m_gate_b"][0], f32)[None, :], (128, 32))),
        "m_conv_w": np.ascontiguousarray(np.asarray(inp["m_conv_w"][0], f32).reshape(5, KC, 128).transpose(2, 1, 0)),
        "m_norm_g": np.ascontiguousarray(np.broadcast_to(np.asarray(inp["m_norm_g"][0], f32)[None, :], (128, D))),
        "masks": MASKS,
        "r_w_in": np.asarray(inp["r_w_in"][0], f32), "r_w_out": np.asarray(inp["r_w_out"][0], f32),
        "r_decay": np.ascontiguousarray(np.broadcast_to(np.asarray(inp["r_decay"][0], f32).reshape(1, 8), (128, 8))),
        "r_norm_g": np.ascontiguousarray(np.broadcast_to(np.asarray(inp["r_norm_g"][0], f32)[None, :], (128, 2048))),
        "r_pos": R_POS, "rope_cs": rope_table(),
    }
    return m


_tri = np.triu(np.ones((128, 128), np.float32))
MASKS = np.ascontiguousarray(np.stack([_tri, _tri.T, -_tri, -_tri.T], axis=0))

_p = np.arange(128, dtype=np.float32)
R_POS = np.ascontiguousarray(np.stack([-(_p + 1), -(128 - _p), (_p + 1), (128 - _p)], axis=1))


def rope_table():
    n_f = 64
    inv = np.power(np.float32(10000.0), -np.arange(n_f, dtype=np.float32) / np.float32(n_f)).astype(np.float32)
    t = np.arange(SEQ)
    row = (t // 64).astype(np.float32)
    col = (t % 64).astype(np.float32)
    ang = np.concatenate([row[:, None] * inv, col[:, None] * inv], axis=-1).astype(np.float32)
    tab = np.zeros((NT, 2, 128), np.float32)
    tab[:CTX, 0] = 1.0
    tab[CTX:, 0] = np.cos(ang)
    tab[CTX:, 1] = np.sin(ang)
    return tab


_CACHE = {}


def kernel(**inputs):
    if "nc" not in _CACHE:
        _CACHE["nc"] = Builder().build()
    nc = _CACHE["nc"]
    in_maps = [host_layout(inputs, b) for b in range(N_CORES)]
    res = run_bass_kernel_spmd(nc, in_maps, core_ids=list(range(N_CORES)))
    out = np.stack([np.ascontiguousarray(res.results[b]["outT"].T) for b in range(N_CORES)], axis=0)
    return out.astype(np.float32)
```

```python
import numpy as np
from contextlib import ExitStack
import concourse.bass as bass
import concourse.mybir as mybir
from concourse.bass_utils import run_bass_kernel_spmd

F32 = mybir.dt.float32
BF16 = mybir.dt.bfloat16
AF = mybir.ActivationFunctionType
ALU = mybir.AluOpType
AX = mybir.AxisListType

D = 1024
KC = 8
CTX = 256
SEQ = 8192
NT = CTX + SEQ
NCH = NT // 128


def set_seq(n):
    global SEQ, NT, NCH
    SEQ = n
    NT = CTX + SEQ
    NCH = NT // 128

DFF = 2816
HC = DFF // 128
NMOD = 9
EPS = 1e-6
M_IN = 3104
R_IN = 6144
N_CORES = 4


class Reg:
    __slots__ = ("w", "r")

    def __init__(self):
        self.w = {}
        self.r = {}


def _merge(d, s):
    for k, v in s.items():
        if d.get(k, 0) < v:
            d[k] = v


class Sy:
    def __init__(self, nc):
        self.nc = nc
        self.sems = {}
        self.nsem = 0
        self.eng = {}
        for n in ("tensor", "vector", "scalar", "gpsimd", "sync"):
            self.eng[n] = dict(h=getattr(nc, n), sem=self._new(), cnt=0, waited={})
        self.dq = {}
        for n in ("sync", "gpsimd", "scalar"):
            self.dq[n] = dict(sems=[self._new() for _ in range(12)], vals=[0] * 12, idx=0)

    def _new(self):
        i = self.nsem
        self.nsem += 1
        self.sems[i] = self.nc.alloc_semaphore(f"sem{i}")
        return i

    def _wait(self, e, deps):
        for s, v in deps.items():
            if e["waited"].get(s, 0) < v:
                e["h"].wait_ge(self.sems[s], v)
                e["waited"][s] = v

    def _deps(self, reads, writes, own=None):
        deps = {}
        for r in reads:
            _merge(deps, r.w)
        for w in writes:
            for src in (w.w, w.r):
                for k, v in src.items():
                    if k == own:
                        continue
                    if deps.get(k, 0) < v:
                        deps[k] = v
        return deps

    def _commit(self, tok, reads, writes):
        for w in writes:
            w.w = {tok[0]: tok[1]}
            w.r = {}
        for r in reads:
            if r.r.get(tok[0], 0) < tok[1]:
                r.r[tok[0]] = tok[1]

    def op(self, en, fn, reads=(), writes=(), inc=True):
        e = self.eng[en]
        if (not inc) and e["cnt"] >= 29900:
            e["sem"] = self._new()
            e["cnt"] = 0
        own = e["sem"] if en != "tensor" or True else None
        self._wait(e, self._deps(reads, writes, own=own))
        ins = fn(e["h"])
        if inc:
            if e["cnt"] >= 30000:
                e["sem"] = self._new()
                e["cnt"] = 0
            e["cnt"] += 1
            ins.then_inc(self.sems[e["sem"]], 1)
            tok = (e["sem"], e["cnt"])
        else:
            if e["cnt"] >= 29900:
                e["sem"] = self._new()
                e["cnt"] = 0
            tok = (e["sem"], e["cnt"] + 1)
        self._commit(tok, reads, writes)

    def dma(self, qn, out, in_, reads=(), writes=()):
        e = self.eng[qn]
        q = self.dq[qn]
        k = q["idx"]
        q["idx"] = (k + 1) % len(q["sems"])
        if q["vals"][k] >= 30000:
            q["sems"][k] = self._new()
            q["vals"][k] = 0
        deps = self._deps(reads, writes)
        if q["vals"][k] > 0:
            _merge(deps, {q["sems"][k]: q["vals"][k]})
        self._wait(e, deps)
        q["vals"][k] += 16
        e["h"].dma_start(out=out, in_=in_).then_inc(self.sems[q["sems"][k]], 16)
        self._commit((q["sems"][k], q["vals"][k]), reads, writes)

    def mm(self, out, pairs, reads, wreg, tr=False):
        n = len(pairs)
        for i, (l, r) in enumerate(pairs):
            self.op("tensor",
                    lambda t, l=l, r=r, i=i: t.matmul(out, lhsT=l, rhs=r, start=(i == 0), stop=(i == n - 1)),
                    reads=reads if i == 0 else (), writes=(wreg,) if i == 0 else (), inc=(i == n - 1))
        if n > 1:
            e = self.eng["tensor"]
            tok = (e["sem"], e["cnt"])
            wreg.w = {tok[0]: tok[1]}
            for r in reads:
                if r.r.get(tok[0], 0) < tok[1]:
                    r.r[tok[0]] = tok[1]

    def barrier(self):
        deps = {}
        for n, e in self.eng.items():
            if e["cnt"] > 0:
                deps[e["sem"]] = e["cnt"]
        for n, q in self.dq.items():
            for k in range(len(q["sems"])):
                if q["vals"][k] > 0:
                    deps[q["sems"][k]] = q["vals"][k]
        for n, e in self.eng.items():
            self._wait(e, dict(deps))

    def finish(self, regs):
        e = self.eng["sync"]
        deps = {}
        for r in regs:
            _merge(deps, r.w)
        self._wait(e, deps)


def run_pipeline(factories, depth=2, stagger=False):
    live = []
    it = iter(factories)
    done = False
    while True:
        started = 0
        while len(live) < depth and not done and not (stagger and started >= 1):
            try:
                live.append(next(it)())
                started += 1
            except StopIteration:
                done = True
        if not live:
            break
        nxt = []
        for g_ in live:
            try:
                next(g_)
                nxt.append(g_)
            except StopIteration:
                pass
        live = nxt


class Builder:
    def __init__(self, stop_after=None):
        self.stop_after = stop_after
        nc = bass.Bass("TRN2", target_bir_lowering=False)
        self.nc = nc
        self.sy = Sy(nc)
        self.regs = {}
        self.inputs()
        self.consts_and_state()

    def R(self, *key):
        r = self.regs.get(key)
        if r is None:
            r = self.regs[key] = Reg()
        return r

    def din(self, name, shape, dt=F32):
        return self.nc.dram_tensor(name, list(shape), dt, kind="ExternalInput").ap()

    def dscr(self, name, shape, dt):
        return self.nc.dram_tensor(name, list(shape), dt, kind="Internal").ap()

    def sb(self, name, shape, dt=F32):
        return self.nc.alloc_sbuf_tensor(name, list(shape), dt)

    def phase_begin(self):
        self.sy.barrier()
        self._stack = ExitStack()
        self._pn = getattr(self, "_pn", 0) + 1

    def tsb(self, name, shape, dt=F32):
        return self._stack.enter_context(self.nc.sbuf_tensor(f"{name}_p{self._pn}", list(shape), dt))

    def phase_end(self):
        self.sy.barrier()
        self._stack.close()
        self._stack = None

    def inputs(self):
        self.xT_in = self.din("xT", [D, NT])
        self.cT = self.din("cT", [128, KC, 2])
        self.mod_w = self.din("mod_w", [2, D, NMOD * D])
        self.mod_b = self.din("mod_b", [2, 128, NMOD * KC])
        self.norm_g = self.din("norm_g", [128, 2 * 3 * KC])
        self.ffn_w13 = self.din("ffn_w13", [2, 2, D, 2 * DFF])
        self.ffn_w2 = self.din("ffn_w2", [2, 2, DFF, D])
        self.final_g = self.din("final_g", [128, KC])
        self.ident_in = self.din("ident", [128, 128])
        self.outT = self.nc.dram_tensor("outT", [D, SEQ], F32, kind="ExternalOutput").ap()
        self.m_w_in = self.din("m_w_in", [D, M_IN])
        self.m_w_out = self.din("m_w_out", [D, D])
        self.m_gate_b = self.din("m_gate_b", [128, 32])
        self.m_conv_w = self.din("m_conv_w", [128, KC, 5])
        self.m_norm_g = self.din("m_norm_g", [128, D])
        self.masks_in = self.din("masks", [4, 128, 128])
        self.r_w_in = self.din("r_w_in", [D, R_IN])
        self.r_w_out = self.din("r_w_out", [2048, D])
        self.r_decay = self.din("r_decay", [128, 8])
        self.r_norm_g = self.din("r_norm_g", [128, 2048])
        self.r_pos = self.din("r_pos", [128, 4])
        self.rope_cs = self.din("rope_cs", [NT, 2, 128])
        self.r_wins = self.dscr("r_wins", [128, KC, R_IN], BF16)
        self.r_wouts = self.dscr("r_wouts", [128, 16, D], BF16)
        self.rqT = self.dscr("rqT", [NCH, 128, D], BF16)
        self.rkT = self.dscr("rkT", [NCH, 128, D], BF16)
        self.rktok = self.dscr("rktok", [NT, D], BF16)
        self.rvtok = self.dscr("rvtok", [NT, 2048], BF16)
        self.rgtok = self.dscr("rgtok", [NT, 2048], F32)
        self.rofw = self.dscr("rofw", [NT, 2048], F32)
        self.robw = self.dscr("robw", [NT, 2048], F32)
        self.m_wins = self.dscr("m_wins", [128, KC, M_IN], BF16)
        self.m_wouts = self.dscr("m_wouts", [128, KC, D], BF16)
        self.uqkT = self.dscr("uqkT", [D, NT], F32)
        self.qT = self.dscr("qT", [512, NT], BF16)
        self.kT = self.dscr("kT", [512, NT], BF16)
        self.vtok = self.dscr("vtok", [NT, D], BF16)
        self.otok = self.dscr("otok", [NT, D], F32)
        self.hfw = self.dscr("hfw", [NT, D], F32)
        self.hbw = self.dscr("hbw", [NT, D], F32)
        self.xs = self.dscr("xs", [D, NT], F32)
        self.w13s = self.dscr("w13s", [2, 2, 128, KC, 2 * DFF], BF16)
        self.w2s = self.dscr("w2s", [2, 2, 128, HC, D], BF16)

    def consts_and_state(self):
        nc, sy = self.nc, self.sy
        self.ps = [nc.alloc_psum_tensor(f"ps{i}", [128, 512], F32) for i in range(8)]
        self.psr = [self.R("ps", i) for i in range(8)]
        self.ones_bf = self.sb("ones_bf", [128, 128], BF16)
        self.ident_f = self.sb("ident_f", [128, 128], F32)
        self.ident_b = self.sb("ident_b", [128, 128], BF16)
        self.modx = self.sb("modx", [128, 2, NMOD * KC])
        self.modc = self.sb("modc", [128, 2, NMOD * KC])
        self.ng = self.sb("ng", [128, 2 * 3 * KC])
        self.fg = self.sb("fg", [128, KC])
        self.cst = self.R("consts")
        self.masks = self.sb("masks_sb", [128, 4, 128])
        self.negones = self.sb("negones", [128, 128])
        self.ones_col = self.sb("ones_col", [128, 2], BF16)
        self.ws_all = self.sb("ws_all", [128, NCH, 16])
        self.thr_all = self.sb("thr_all", [128, NCH, 16])
        self.ebl_all = self.sb("ebl_all", [128, NCH, 16])
        for m in range(4):
            sy.dma("sync", self.masks[:, m, :], self.masks_in[m], writes=(self.cst,))
        sy.op("vector", lambda v: v.memset(self.negones[:, :], -1.0), writes=(self.cst,))
        sy.op("vector", lambda v: v.memset(self.ones_col[:, :], 1.0), writes=(self.cst,))
        self._mn = dict(sq=[self.sb(f"mn_sq{b}", [128, 512], BF16) for b in range(2)],
                        tmp=[self.sb(f"mn_tmp{b}", [128, 512]) for b in range(2)],
                        rstd=self.sb("mn_rstd", [128, 512]), n=0)
        sy.op("vector", lambda v: v.memset(self.ones_bf[:, :], 1.0 / 1024.0), writes=(self.cst,))
        sy.dma("sync", self.ident_f[:, :], self.ident_in, writes=(self.cst,))
        sy.op("vector", lambda v: v.tensor_copy(out=self.ident_b[:, :], in_=self.ident_f[:, :]),
              reads=(self.cst,), writes=(self.R("identb"),))
        sy.dma("sync", self.ng[:, :], self.norm_g, writes=(self.cst,))
        sy.dma("sync", self.fg[:, :], self.final_g, writes=(self.cst,))

    def precast_ffn(self, i, j):
        sy = self.sy
        for k in range(KC):
            sy.dma("gpsimd", self.w13s[i, j, :, k, :], self.ffn_w13[i, j, k * 128:(k + 1) * 128, :], writes=(self.R("w13s", i, j, k),))
        for k in range(HC):
            sy.dma("gpsimd", self.w2s[i, j, :, k, :], self.ffn_w2[i, j, k * 128:(k + 1) * 128, :], writes=(self.R("w2s", i, j, k),))

    def compute_mod(self, layers=(0, 1)):
        nc, sy = self.nc, self.sy
        self.phase_begin()
        s_raw = self.tsb("s_raw", [128, KC, 2])
        s_act = self.tsb("s_act", [128, KC, 2])
        mb = self.tsb("mb", [128, 2, NMOD * KC])
        r_s = self.R("s_act")
        sy.dma("sync", s_raw[:, :, :], self.cT, writes=(r_s,))
        sy.op("scalar", lambda a: a.activation(out=s_act[:, :, :], in_=s_raw[:, :, :], func=AF.Silu),
              reads=(r_s,), writes=(r_s,))
        r_mb = self.R("mb")
        for i in layers:
            sy.dma("sync", mb[:, i, :], self.mod_b[i], writes=(r_mb,))
        NB = 512
        wbuf = [self.tsb(f"modw{b}", [128, KC, NB]) for b in range(2)]
        rw = [self.R("modw", b) for b in range(2)]
        r_mod = self.R("mod")
        pst = self.ps[0]
        n = 0
        for i in layers:
            for blk in range(NMOD * D // NB):
                b = n % 2
                n += 1
                sy.dma("sync", wbuf[b][:, :, :],
                       self.mod_w[i, :, blk * NB:(blk + 1) * NB].rearrange("(k p) n -> p k n", p=128),
                       writes=(rw[b],))
                for fc in range(NB // 128):
                    col = (blk * NB) // 128 + fc
                    first = (blk == 0 and fc == 0)
                    sy.mm(pst[:, 2 * col:2 * col + 2],
                          [(wbuf[b][:, k, fc * 128:(fc + 1) * 128], s_act[:, k, :]) for k in range(KC)],
                          reads=(rw[b], r_s), wreg=self.psr[0] if first else Reg())
            e = sy.eng["tensor"]
            self.psr[0].w = {e["sem"]: e["cnt"]}
            pv = pst[:, 0:2 * NMOD * KC].rearrange("p (c t) -> p c t", t=2)
            sy.op("vector", lambda v, i=i, pv=pv: v.tensor_tensor(out=self.modx[:, i, :], in0=pv[:, :, 0], in1=mb[:, i, :], op=ALU.add),
                  reads=(self.psr[0], r_mb), writes=(r_mod,))
            sy.op("vector", lambda v, i=i, pv=pv: v.tensor_tensor(out=self.modc[:, i, :], in0=pv[:, :, 1], in1=mb[:, i, :], op=ALU.add),
                  reads=(self.psr[0], r_mb), writes=(r_mod,))
        self.r_mod = r_mod
        self.phase_end()

    def sub_mod(self, i, j, res):
        sy = self.sy
        key = (i, j)
        gs = self.sb(f"gs{i}{j}", [128, 2, KC])
        hg = self.sb(f"hg{i}{j}", [128, 2, KC])
        sh = []
        r = self.R("submod", i, j)
        for v_, m in enumerate((self.modx, self.modc)):
            sc = m[:, i, (3 * j + 1) * KC:(3 * j + 2) * KC]
            gt = m[:, i, (3 * j + 2) * KC:(3 * j + 3) * KC]
            g = self.ng[:, (i * 3 + j) * KC:(i * 3 + j + 1) * KC]
            sy.op("vector", lambda v, v_=v_, sc=sc, g=g: v.scalar_tensor_tensor(
                out=gs[:, v_, :], in0=sc, scalar=1.0, in1=g, op0=ALU.add, op1=ALU.mult),
                reads=(self.r_mod, self.cst), writes=(r,))
            sy.op("vector", lambda v, v_=v_, gt=gt: v.tensor_scalar(
                out=hg[:, v_, :], in0=gt, scalar1=float(res), scalar2=None, op0=ALU.mult),
                reads=(self.r_mod,), writes=(r,))
            sh.append(m[:, i, (3 * j) * KC:(3 * j + 1) * KC])
        return gs, sh, hg, r

    def rms_stats(self, xt, xr, col0, w):
        sy = self.sy
        mn = self._mn
        bank = 7
        for k in range(KC):
            b = mn["n"] % 2
            mn["n"] += 1
            sq = mn["sq"][b]
            sy.op("scalar", lambda a, sq=sq, k=k: a.activation(out=sq[:, 0:w], in_=xt[:, k, col0:col0 + w], func=AF.Square),
                  reads=(xr,), writes=(self.R("mn_sq", b),))
            sy.op("tensor", lambda t, sq=sq, k=k: t.matmul(self.ps[bank][:, 0:w], lhsT=self.ones_bf[:, :], rhs=sq[:, 0:w],
                                                          start=(k == 0), stop=(k == KC - 1)),
                  reads=(self.R("mn_sq", b), self.cst), writes=(self.psr[bank],) if k == 0 else (), inc=True)
        e = sy.eng["tensor"]
        self.psr[bank].w = {e["sem"]: e["cnt"]}
        sy.op("scalar", lambda a: a.activation(out=mn["rstd"][:, 0:w], in_=self.ps[bank][:, 0:w], func=AF.Ln, bias=EPS, scale=1.0),
              reads=(self.psr[bank],), writes=(self.R("mn_rstd"),))
        sy.op("scalar", lambda a: a.activation(out=mn["rstd"][:, 0:w], in_=mn["rstd"][:, 0:w], func=AF.Exp, scale=-0.5),
              reads=(self.R("mn_rstd"),), writes=(self.R("mn_rstd"),))

    def modnorm_piece(self, xt, xr, col0, w, kind, gs, sh, rmod, hT, hr, hcol0):
        sy = self.sy
        v_ = 0 if kind == "x" else 1
        self.rms_stats(xt, xr, col0, w)
        mn = self._mn
        rs_r = self.R("mn_rstd")
        for k in range(KC):
            b = mn["n"] % 2
            mn["n"] += 1
            tmp = mn["tmp"][b]
            sy.op("vector", lambda v, tmp=tmp, k=k: v.scalar_tensor_tensor(
                out=tmp[:, 0:w], in0=xt[:, k, col0:col0 + w], scalar=gs[:, v_, k:k + 1], in1=mn["rstd"][:, 0:w],
                op0=ALU.mult, op1=ALU.mult),
                reads=(xr, rs_r, rmod), writes=(self.R("mn_tmp", b),))
            sy.op("scalar", lambda a, tmp=tmp, k=k: a.activation(
                out=hT[:, k, hcol0:hcol0 + w], in_=tmp[:, 0:w], func=AF.Identity, bias=sh[v_][:, k:k + 1], scale=1.0),
                reads=(self.R("mn_tmp", b), self.r_mod), writes=(hr,))

    def supertiles(self):
        sts = [[(0, 256, "c")]]
        for s in range(SEQ // 1024):
            b = 256 + s * 1024
            sts.append([(b, 512, "x"), (b + 512, 512, "x")])
        return sts[:getattr(self, "dbg_nst", 99)]

    def ffn(self, i, j, src, final=False):
        nc, sy = self.nc, self.sy
        jm = 0 if j == 0 else 2
        gs, sh, hg, rmod = self.sub_mod(i, jm, 0.5)
        self.phase_begin()
        xt = self.tsb("f_xt", [128, KC, 1024])
        hT = [self.tsb(f"f_hT{b}", [128, KC, 1024], BF16) for b in range(2)]
        g = self.tsb("f_g", [128, HC, 1024], BF16)
        w13 = [self.tsb(f"f_w13_{b}", [128, KC, 512], BF16) for b in range(2)]
        w2q = [self.tsb(f"f_w2_{b}", [128, HC, 256], BF16) for b in range(2)]
        sab = [self.tsb(f"f_sa{b}", [128, 512]) for b in range(2)]
        xr = [self.tsb(f"f_xr{b}", [128, 512]) for b in range(3)]
        r_xt, r_g = Reg(), Reg()
        rw2q = [Reg(), Reg()]
        r_hT = [Reg(), Reg()]
        rw13 = [Reg(), Reg()]
        rsa_ = [Reg(), Reg()]
        r_xr = [Reg(), Reg(), Reg()]
        w13rs = tuple(self.R("w13s", i, j, k) for k in range(KC))
        w2rs = tuple(self.R("w2s", i, j, k) for k in range(HC))
        sts = self.supertiles()
        cnt = dict(n13=0, nsa=0, nps=0, nys=0, nxr=0, nw2=0)

        def load_w2(oq):
            bq = cnt["nw2"] % 2
            cnt["nw2"] += 1
            sy.dma("sync", w2q[bq][:, :, :], self.w2s[i, j, :, :, oq * 256:(oq + 1) * 256], reads=w2rs, writes=(rw2q[bq],))
            return bq

        def gen_norm(si):
            st = sts[si]
            base = st[0][0]
            tot = sum(p[1] for p in st)
            for k in range(KC):
                sy.dma("sync", xt[:, k, 0:tot], src[k * 128:(k + 1) * 128, base:base + tot], writes=(r_xt,))
            yield
            for (c0, w, kind) in st:
                v_ = 0 if kind == "x" else 1
                lo = c0 - base
                mn = self._mn
                for k in range(KC):
                    b = mn["n"] % 2
                    mn["n"] += 1
                    sq = mn["sq"][b]
                    sy.op("scalar", lambda a, sq=sq, k=k: a.activation(out=sq[:, 0:w], in_=xt[:, k, lo:lo + w], func=AF.Square),
                          reads=(r_xt,), writes=(self.R("mn_sq", b),))
                    sy.op("tensor", lambda t, sq=sq, k=k: t.matmul(self.ps[7][:, 0:w], lhsT=self.ones_bf[:, :], rhs=sq[:, 0:w],
                                                                  start=(k == 0), stop=(k == KC - 1)),
                          reads=(self.R("mn_sq", b), self.cst), writes=(self.psr[7],) if k == 0 else (), inc=True)
                    if k % 2 == 1:
                        yield
                e = sy.eng["tensor"]
                self.psr[7].w = {e["sem"]: e["cnt"]}
                sy.op("scalar", lambda a: a.activation(out=mn["rstd"][:, 0:w], in_=self.ps[7][:, 0:w], func=AF.Ln, bias=EPS, scale=1.0),
                      reads=(self.psr[7],), writes=(self.R("mn_rstd"),))
                sy.op("scalar", lambda a: a.activation(out=mn["rstd"][:, 0:w], in_=mn["rstd"][:, 0:w], func=AF.Exp, scale=-0.5),
                      reads=(self.R("mn_rstd"),), writes=(self.R("mn_rstd"),))
                yield
                for k in range(KC):
                    b = mn["n"] % 2
                    mn["n"] += 1
                    tmp = mn["tmp"][b]
                    sy.op("vector", lambda v, tmp=tmp, k=k: v.scalar_tensor_tensor(
                        out=tmp[:, 0:w], in0=xt[:, k, lo:lo + w], scalar=gs[:, v_, k:k + 1], in1=mn["rstd"][:, 0:w],
                        op0=ALU.mult, op1=ALU.mult),
                        reads=(r_xt, self.R("mn_rstd"), rmod), writes=(self.R("mn_tmp", b),))
                    sy.op("scalar", lambda a, tmp=tmp, k=k: a.activation(
                        out=hT[si % 2][:, k, lo:lo + w], in_=tmp[:, 0:w], func=AF.Identity, bias=sh[v_][:, k:k + 1], scale=1.0),
                        reads=(self.R("mn_tmp", b), self.r_mod), writes=(r_hT[si % 2],))
                    if k % 2 == 1:
                        yield

        def gen_main(si):
            st = sts[si]
            base = st[0][0]
            h_ = hT[si % 2]
            rh = r_hT[si % 2]
            nxt_bq = load_w2(0)
            for hb in range(11):
                bb = cnt["n13"] % 2
                cnt["n13"] += 1
                wt = w13[bb]
                rw = rw13[bb]
                for half in range(2):
                    c_ = half * DFF + hb * 256
                    sy.dma("sync", wt[:, :, half * 256:(half + 1) * 256], self.w13s[i, j, :, :, c_:c_ + 256],
                           reads=w13rs, writes=(rw,))
                for (c0, w, kind) in st:
                    lo = c0 - base
                    for h2 in range(2):
                        pa = 2 * (cnt["nps"] % 2)
                        cnt["nps"] += 1
                        pb = pa + 1
                        sy.mm(self.ps[pa][:, 0:w], [(wt[:, k, h2 * 128:(h2 + 1) * 128], h_[:, k, lo:lo + w]) for k in range(KC)],
                              reads=(rw, rh), wreg=self.psr[pa])
                        sy.mm(self.ps[pb][:, 0:w], [(wt[:, k, 256 + h2 * 128:256 + (h2 + 1) * 128], h_[:, k, lo:lo + w]) for k in range(KC)],
                              reads=(rw, rh), wreg=self.psr[pb])
                        sb_ = cnt["nsa"] % 2
                        cnt["nsa"] += 1
                        sa = sab[sb_]
                        rsa = rsa_[sb_]
                        sy.op("scalar", lambda a, sa=sa, pa=pa, w=w: a.activation(out=sa[:, 0:w], in_=self.ps[pa][:, 0:w], func=AF.Silu),
                              reads=(self.psr[pa],), writes=(rsa,))
                        hc = hb * 2 + h2
                        sy.op("vector", lambda v, sa=sa, pb=pb, hc=hc, lo=lo, w=w: v.tensor_tensor(
                            out=g[:, hc, lo:lo + w], in0=sa[:, 0:w], in1=self.ps[pb][:, 0:w], op=ALU.mult),
                            reads=(rsa, self.psr[pb]), writes=(r_g,))
                yield
            for oq in range(4):
                bq = nxt_bq
                if oq + 1 < 4:
                    nxt_bq = load_w2(oq + 1)
                w2 = w2q[bq]
                rw2 = rw2q[bq]
                for o2 in range(2):
                    oc = oq * 2 + o2
                    for (c0, w, kind) in st:
                        lo = c0 - base
                        v_ = 0 if kind == "x" else 1
                        xb = cnt["nxr"] % 3
                        cnt["nxr"] += 1
                        sy.dma("sync", xr[xb][:, 0:w], src[oc * 128:(oc + 1) * 128, c0:c0 + w], writes=(r_xr[xb],))
                        py = 4 + (cnt["nys"] % 3)
                        cnt["nys"] += 1
                        sy.mm(self.ps[py][:, 0:w], [(w2[:, k, o2 * 128:(o2 + 1) * 128], g[:, k, lo:lo + w]) for k in range(HC)],
                              reads=(rw2, r_g), wreg=self.psr[py])
                        sy.op("vector", lambda v, py=py, oc=oc, v_=v_, w=w, xb=xb: v.scalar_tensor_tensor(
                            out=xr[xb][:, 0:w], in0=self.ps[py][:, 0:w], scalar=hg[:, v_, oc:oc + 1], in1=xr[xb][:, 0:w],
                            op0=ALU.mult, op1=ALU.add),
                            reads=(self.psr[py], rmod, r_xr[xb]), writes=(r_xr[xb],))
                        sy.dma("scalar", self.xs[oc * 128:(oc + 1) * 128, c0:c0 + w], xr[xb][:, 0:w], reads=(r_xr[xb],), writes=(Reg(),))
                    yield

        for _ in gen_norm(0):
            pass
        for si in range(len(sts)):
            ga = gen_main(si)
            gb = gen_norm(si + 1) if si + 1 < len(sts) else iter(())
            a_alive = b_alive = True
            while a_alive or b_alive:
                if a_alive:
                    try:
                        next(ga)
                    except StopIteration:
                        a_alive = False
                for _ in range(2):
                    if b_alive:
                        try:
                            next(gb)
                        except StopIteration:
                            b_alive = False
        self.phase_end()
        if final:
            self.final_norm()

    def final_norm(self):
        sy = self.sy
        self.phase_begin()
        xt = [self.tsb(f"n_xt{b}", [128, KC, 512]) for b in range(2)]
        r_xt = [Reg(), Reg()]
        r_out = self.R("outT")
        mn = self._mn
        w = 512

        def body(pi):
            b = pi % 2
            c0 = CTX + pi * 512
            for k in range(KC):
                sy.dma("sync", xt[b][:, k, :], self.xs[k * 128:(k + 1) * 128, c0:c0 + w], writes=(r_xt[b],))
            yield
            self.rms_stats(xt[b], r_xt[b], 0, w)
            yield
            for k in range(KC):
                sy.op("vector", lambda v, k=k, b=b: v.scalar_tensor_tensor(
                    out=xt[b][:, k, :], in0=xt[b][:, k, :], scalar=self.fg[:, k:k + 1], in1=mn["rstd"][:, 0:w],
                    op0=ALU.mult, op1=ALU.mult),
                    reads=(r_xt[b], self.R("mn_rstd"), self.cst), writes=(r_xt[b],))
            for k in range(KC):
                sy.dma("scalar", self.outT[k * 128:(k + 1) * 128, c0 - CTX:c0 - CTX + w], xt[b][:, k, :], reads=(r_xt[b],), writes=(r_out,))

        run_pipeline([(lambda pi=pi: body(pi)) for pi in range(SEQ // 512)], depth=2, stagger=True)
        self.phase_end()

    def precast_mlstm(self):
        sy = self.sy
        r = self.R("m_wins")
        for k in range(KC):
            sy.dma("gpsimd", self.m_wins[:, k, :], self.m_w_in[k * 128:(k + 1) * 128, :], writes=(r,))
        r2 = self.R("m_wouts")
        for k in range(KC):
            sy.dma("gpsimd", self.m_wouts[:, k, :], self.m_w_out[k * 128:(k + 1) * 128, :], writes=(r2,))

    def pieces(self):
        ps_ = [(0, 256, "c")]
        for s in range(SEQ // 512):
            ps_.append((256 + s * 512, 512, "x"))
        return ps_

    def mlstm_inproj(self, i):
        sy = self.sy
        gs, sh, hg, rmod = self.sub_mod(i, 1, 1.0)
        self.m_hg = hg
        self.m_rmod = rmod
        self.phase_begin()
        win = self.tsb("m_win", [128, KC, M_IN], BF16)
        xt = self.tsb("m_xt", [128, KC, 512])
        hT = self.tsb("m_hT", [128, KC, 512], BF16)
        stq = [self.tsb(f"m_stq{b}", [128, 512]) for b in range(2)]
        stv = [self.tsb(f"m_stv{b}", [128, D], BF16) for b in range(2)]
        sto = [self.tsb(f"m_sto{b}", [128, D]) for b in range(2)]
        gb = self.tsb("m_gb", [128, 32])
        gpre = self.tsb("m_gpre", [128, 4, 8])
        lt = self.tsb("m_lt", [128, 2, 8])
        dtmp = self.tsb("m_dtmp", [128, 2, 8])
        r_win, r_xt, r_hT = Reg(), Reg(), Reg()
        r_stq, r_stv, r_sto = [Reg(), Reg()], [Reg(), Reg()], [Reg(), Reg()]
        r_gb, r_gpre, r_lt, r_dt = Reg(), Reg(), Reg(), Reg()
        r_xs = self.R("xT", id(self.xs))
        r_uqk, r_v, r_o, r_gate = self.R("uqkT"), self.R("vtok"), self.R("otok"), self.R("gates")
        for k in range(KC):
            sy.dma("sync", win[:, k, :], self.m_wins[:, k, :], reads=(self.R("m_wins"),), writes=(r_win,))
        sy.dma("sync", gb[:, :], self.m_gate_b, writes=(r_gb,))
        nq = nv = 0
        for (c0, w, kind) in self.pieces():
            for k in range(KC):
                sy.dma("sync", xt[:, k, 0:w], self.xs[k * 128:(k + 1) * 128, c0:c0 + w], reads=(r_xs,), writes=(r_xt,))
            self.modnorm_piece(xt, r_xt, 0, w, kind, gs, sh, rmod, hT, r_hT, 0)
            for oc in range(KC):
                pb = nq % 2
                b = nq % 2
                nq += 1
                sy.mm(self.ps[pb][:, 0:w], [(win[:, k, oc * 128:(oc + 1) * 128], hT[:, k, 0:w]) for k in range(KC)],
                      reads=(r_win, r_hT), wreg=self.psr[pb])
                sy.op("scalar", lambda a, b=b, pb=pb, w=w: a.copy(out=stq[b][:, 0:w], in_=self.ps[pb][:, 0:w]),
                      reads=(self.psr[pb],), writes=(r_stq[b],))
                sy.dma("gpsimd", self.uqkT[oc * 128:(oc + 1) * 128, c0:c0 + w], stq[b][:, 0:w], reads=(r_stq[b],), writes=(r_uqk,))
            def body(t4, c0=c0, w=w):
                nonlocal nv
                c = c0 // 128 + t4
                tsl = slice(t4 * 128, (t4 + 1) * 128)
                b = nv % 2
                nv += 1
                for n2 in range(2):
                    pb = 2 + n2
                    sy.mm(self.ps[pb][:, :], [(hT[:, k, tsl], win[:, k, 1024 + n2 * 512:1024 + (n2 + 1) * 512]) for k in range(KC)],
                          reads=(r_win, r_hT), wreg=self.psr[pb])
                    sy.op("vector", lambda v, b=b, pb=pb, n2=n2: v.tensor_copy(out=stv[b][:, n2 * 512:(n2 + 1) * 512], in_=self.ps[pb][:, :]),
                          reads=(self.psr[pb],), writes=(r_stv[b],))
                sy.dma("gpsimd", self.vtok[c * 128:(c + 1) * 128, :], stv[b][:, :], reads=(r_stv[b],), writes=(r_v,))
                yield
                for n2 in range(2):
                    pb = 4 + n2
                    sy.mm(self.ps[pb][:, :], [(hT[:, k, tsl], win[:, k, 2048 + n2 * 512:2048 + (n2 + 1) * 512]) for k in range(KC)],
                          reads=(r_win, r_hT), wreg=self.psr[pb])
                    sy.op("scalar", lambda a, b=b, pb=pb, n2=n2: a.copy(out=sto[b][:, n2 * 512:(n2 + 1) * 512], in_=self.ps[pb][:, :]),
                          reads=(self.psr[pb],), writes=(r_sto[b],))
                sy.dma("gpsimd", self.otok[c * 128:(c + 1) * 128, :], sto[b][:, :], reads=(r_sto[b],), writes=(r_o,))
                yield
                pg = 6
                sy.mm(self.ps[pg][:, 0:32], [(hT[:, k, tsl], win[:, k, 3072:3104]) for k in range(KC)],
                      reads=(r_win, r_hT), wreg=self.psr[pg])
                sy.op("vector", lambda v: v.tensor_tensor(out=gpre[:, :, :].rearrange("p a h -> p (a h)"), in0=self.ps[pg][:, 0:32], in1=gb[:, :], op=ALU.add),
                      reads=(self.psr[pg], r_gb), writes=(r_gpre,))
                sy.op("scalar", lambda a: a.activation(out=lt[:, :, :], in_=gpre[:, 1::2, :], func=AF.Exp, scale=-1.0),
                      reads=(r_gpre,), writes=(r_lt,))
                sy.op("scalar", lambda a: a.activation(out=lt[:, :, :], in_=lt[:, :, :], func=AF.Ln, bias=1.0, scale=1.0),
                      reads=(r_lt,), writes=(r_lt,))
                yield
                sy.mm(self.ps[pg][:, 64:72], [(self.masks[:, 2, :], lt[:, 0, :])], reads=(r_lt, self.cst), wreg=self.psr[pg])
                sy.mm(self.ps[pg][:, 72:80], [(self.masks[:, 3, :], lt[:, 1, :])], reads=(r_lt, self.cst), wreg=self.psr[pg])
                sy.mm(self.ps[pg][:, 96:112], [(self.negones[:, :], lt[:, :, :].rearrange("p a h -> p (a h)"))], reads=(r_lt, self.cst), wreg=self.psr[pg])
                bps = self.ps[pg][:, 64:80].rearrange("p (a h) -> p a h", a=2)
                sy.op("vector", lambda v, bps=bps: v.tensor_tensor(out=dtmp[:, :, :], in0=gpre[:, 0::2, :], in1=bps, op=ALU.subtract),
                      reads=(r_gpre, self.psr[pg]), writes=(r_dt,))
                rg = self.R("gstat", c)
                sy.op("scalar", lambda a, c=c: a.activation(out=self.ws_all[:, c, :], in_=dtmp[:, :, :].rearrange("p a h -> p (a h)"), func=AF.Exp,
                                                          bias=float(np.log(0.125)), scale=1.0),
                      reads=(r_dt,), writes=(rg,))
                sy.op("scalar", lambda a, c=c: a.activation(out=self.thr_all[:, c, :], in_=self.ps[pg][:, 64:80], func=AF.Exp, scale=-1.0),
                      reads=(self.psr[pg],), writes=(rg,))
                sy.op("scalar", lambda a, c=c: a.activation(out=self.ebl_all[:, c, :], in_=self.ps[pg][:, 96:112], func=AF.Exp),
                      reads=(self.psr[pg],), writes=(rg,))

            run_pipeline([(lambda t4=t4: body(t4)) for t4 in range(w // 128)], depth=2, stagger=True)
        self.phase_end()

    def mlstm_conv(self):
        sy = self.sy
        self.phase_begin()
        W = 1024
        cw = self.tsb("c_cw", [128, KC, 5])
        u = [self.tsb(f"c_u{b}", [128, W + 4]) for b in range(2)]
        acc = [self.tsb(f"c_acc{b}", [128, W]) for b in range(2)]
        ob = [self.tsb(f"c_ob{b}", [128, W], BF16) for b in range(2)]
        r_cw = Reg()
        r_u, r_acc, r_ob = [Reg(), Reg()], [Reg(), Reg()], [Reg(), Reg()]
        r_uqk, r_q, r_k = self.R("uqkT"), self.R("qT"), self.R("kT")
        sy.dma("sync", cw[:, :, :], self.m_conv_w, writes=(r_cw,))
        blocks = [(0, 256, 0, 256)] + [(256 + s * W, W, 256, NT) for s in range(SEQ // W)]
        n = 0
        for oc in range(KC):
            for (c0, w, lo_lim, hi_lim) in blocks:
                b = n % 2
                n += 1
                hl = 2 if c0 - 2 >= lo_lim else 0
                hr = 2 if c0 + w + 2 <= hi_lim else 0
                if hl == 0:
                    sy.op("vector", lambda v, b=b: v.memset(u[b][:, 0:2], 0.0), writes=(r_u[b],))
                if hr == 0:
                    sy.op("vector", lambda v, b=b, w=w: v.memset(u[b][:, w + 2:w + 4], 0.0), writes=(r_u[b],))
                sy.dma("sync", u[b][:, 2 - hl:2 + w + hr], self.uqkT[oc * 128:(oc + 1) * 128, c0 - hl:c0 + w + hr],
                       reads=(r_uqk,), writes=(r_u[b],))
                sy.op("vector", lambda v, b=b, w=w, oc=oc: v.tensor_scalar(out=acc[b][:, 0:w], in0=u[b][:, 0:w], scalar1=cw[:, oc, 0:1], scalar2=None, op0=ALU.mult),
                      reads=(r_u[b], r_cw), writes=(r_acc[b],))
                for j in range(1, 5):
                    sy.op("vector", lambda v, b=b, w=w, oc=oc, j=j: v.scalar_tensor_tensor(
                        out=acc[b][:, 0:w], in0=u[b][:, j:j + w], scalar=cw[:, oc, j:j + 1], in1=acc[b][:, 0:w], op0=ALU.mult, op1=ALU.add),
                        reads=(r_u[b], r_cw, r_acc[b]), writes=(r_acc[b],))
                sy.op("scalar", lambda a, b=b, w=w: a.activation(out=ob[b][:, 0:w], in_=acc[b][:, 0:w], func=AF.Silu),
                      reads=(r_acc[b],), writes=(r_ob[b],))
                dst = self.qT if oc < 4 else self.kT
                sy.dma("gpsimd", dst[(oc % 4) * 128:(oc % 4 + 1) * 128, c0:c0 + w], ob[b][:, 0:w], reads=(r_ob[b],),
                       writes=(r_q if oc < 4 else r_k,))
        self.phase_end()

    def mlstm_scan(self, i):
        sy = self.sy
        self.phase_begin()
        qTc = [self.tsb(f"s_q{b}", [128, 4, 128], BF16) for b in range(4)]
        kTc = [self.tsb(f"s_k{b}", [128, 4, 128], BF16) for b in range(4)]
        vtc = [self.tsb(f"s_v{b}", [128, D], BF16) for b in range(4)]
        kTm = [self.tsb(f"s_kTm{b}", [128, 8, 128], BF16) for b in range(2)]
        kw = [self.tsb(f"s_kw{b}", [128, 512], BF16) for b in range(2)]
        Pp = [self.tsb(f"s_P{b}", [128, 8, 128], BF16) for b in range(2)]
        hout = [self.tsb(f"s_hout{b}", [128, 8, 128]) for b in range(2)]
        sm = [self.tsb(f"s_sm{b}", [128, 32]) for b in range(2)]
        C = [self.tsb(f"s_C{d}", [128, 4, 128]) for d in range(2)]
        Ctmp = [self.tsb(f"s_Ctmp{d}", [128, 4, 128]) for d in range(2)]
        Cbf = [self.tsb(f"s_Cbf{d}", [128, 8, 128], BF16) for d in range(2)]
        nst = [self.tsb(f"s_n{d}", [128, 4]) for d in range(2)]
        ntmp = [self.tsb(f"s_ntmp{d}", [128, 4]) for d in range(2)]
        nbf = [self.tsb(f"s_nbf{d}", [128, 8, 2], BF16) for d in range(2)]
        r_kTm, r_kw, r_P, r_hout, r_sm = ([Reg(), Reg()] for _ in range(5))
        r_q, r_k, r_v = ([Reg() for _ in range(4)] for _ in range(3))
        r_C, r_Ct, r_Cbf, r_n, r_nt, r_nbf = ([Reg(), Reg()] for _ in range(6))
        r_h = [self.R("hfw"), self.R("hbw")]
        hdst = [self.hfw, self.hbw]
        qTv = self.qT.rearrange("(i p) t -> p i t", p=128)
        kTv = self.kT.rearrange("(i p) t -> p i t", p=128)
        nctx = CTX // 128
        orders = [list(range(NCH)), list(range(nctx - 1, -1, -1)) + list(range(NCH - 1, nctx - 1, -1))]
        for b in range(2):
            sy.op("vector", lambda v, b=b: v.memset(kTm[b][:, :, :], 0.0), writes=(r_kTm[b],))
        for d in range(2):
            sy.op("vector", lambda v, d=d: v.memset(C[d][:, :, :], 0.0), writes=(r_C[d],))
            sy.op("vector", lambda v, d=d: v.memset(Cbf[d][:, :, :], 0.0), writes=(r_Cbf[d],))
            sy.op("vector", lambda v, d=d: v.memset(nst[d][:, :], 0.0), writes=(r_n[d],))
            sy.op("vector", lambda v, d=d: v.memset(nbf[d][:, :, :], 0.0), writes=(r_nbf[d],))
        n = 0
        def body(step, d):
            if True:
                c = orders[d][step]
                b = d
                psA = psD = 1 + 3 * d
                psS = psN = psC = (2 + 3 * d, 3 + 3 * d)
                DO = 256
                lb = 2 * d + step % 2
                tok = slice(c * 128, (c + 1) * 128)
                rg = self.R("gstat", c)
                ws = self.ws_all[:, c, d * 8:(d + 1) * 8]
                thr = self.thr_all[:, c, d * 8:(d + 1) * 8]
                ebl = self.ebl_all[:, c, d * 8:(d + 1) * 8]
                sy.dma("sync", qTc[lb][:, :, :], qTv[:, :, tok], reads=(self.R("qT"),), writes=(r_q[lb],))
                sy.dma("sync", kTc[lb][:, :, :], kTv[:, :, tok], reads=(self.R("kT"),), writes=(r_k[lb],))
                sy.dma("sync", vtc[lb][:, :], self.vtok[tok, :], reads=(self.R("vtok"),), writes=(r_v[lb],))
                yield
                pA = self.ps[psA][:, :].bitcast(BF16)
                for i4 in range(4):
                    sy.op("tensor", lambda t, lb=lb, i4=i4, b=b, pA=pA: t.transpose(pA[:, i4 * 128:(i4 + 1) * 128], kTc[lb][:, i4, :], self.ident_b[:, :]),
                          reads=(r_k[lb], self.R("identb")), writes=(self.psr[psA],) if i4 == 0 else (), inc=(i4 == 3))
                e = sy.eng["tensor"]
                self.psr[psA].w = {e["sem"]: e["cnt"]}
                sy.op("vector", lambda v, pA=pA, ws=ws, b=b: v.tensor_tensor(
                    out=kw[b][:, :].rearrange("p (h e) -> p h e", h=8), in0=pA[:, 0:512].rearrange("p (h e) -> p h e", h=8),
                    in1=ws.unsqueeze(2).to_broadcast([128, 8, 64]), op=ALU.mult),
                    reads=(self.psr[psA], rg), writes=(r_kw[b],))
                for hh in range(2):
                    rows = slice(hh * 64, hh * 64 + 64)
                    sy.op("vector", lambda g, lb=lb, rows=rows, hh=hh, b=b: g.tensor_copy(out=kTm[b][rows, hh::2, :], in_=kTc[lb][rows, :, :]),
                          reads=(r_k[lb],), writes=(r_kTm[b],))
                yield
                for h in range(8):
                    bank = psS[h // 4]
                    col = slice((h % 4) * 128, (h % 4 + 1) * 128)
                    sy.mm(self.ps[bank][:, col], [(kTm[b][:, h, :], qTc[lb][:, h // 2, :])], reads=(r_kTm[b], r_q[lb]),
                          wreg=self.psr[bank] if h % 4 == 0 else Reg())
                    if h % 4 == 3:
                        e = sy.eng["tensor"]
                        self.psr[bank].w = {e["sem"]: e["cnt"]}
                yield
                for h in range(8):
                    bank = psS[h // 4]
                    col = slice((h % 4) * 128, (h % 4 + 1) * 128)
                    sy.op("vector", lambda v, h=h, bank=bank, col=col, ws=ws, d=d, b=b: v.scalar_tensor_tensor(
                        out=Pp[b][:, h, :], in0=self.ps[bank][:, col], scalar=ws[:, h:h + 1], in1=self.masks[:, d, :], op0=ALU.mult, op1=ALU.mult),
                        reads=(self.psr[bank], rg, self.cst), writes=(r_P[b],))
                yield
                for h in range(8):
                    bank = psN[h // 4]
                    col = slice((h % 4) * 128, (h % 4 + 1) * 128)
                    sy.mm(self.ps[bank][:, col], [(qTc[lb][:, h // 2, :], Cbf[d][:, h, :]), (Pp[b][:, h, :], vtc[lb][:, h * 128:(h + 1) * 128])],
                          reads=(r_q[lb], r_Cbf[d], r_P[b], r_v[lb]), wreg=self.psr[bank] if h % 4 == 0 else Reg())
                    if h % 4 == 3:
                        e = sy.eng["tensor"]
                        self.psr[bank].w = {e["sem"]: e["cnt"]}
                for h in range(8):
                    sy.mm(self.ps[psD][:, DO + 2 * h:DO + 2 * h + 2], [(qTc[lb][:, h // 2, :], nbf[d][:, h, :]), (Pp[b][:, h, :], self.ones_col[:, :])],
                          reads=(r_q[lb], r_nbf[d], r_P[b], self.cst), wreg=self.psr[psD] if h == 0 else Reg())
                e = sy.eng["tensor"]
                self.psr[psD].w = {e["sem"]: e["cnt"]}
                yield
                smb = sm[b]
                sy.op("vector", lambda v, smb=smb: v.tensor_scalar(out=smb[:, 0:8], in0=self.ps[psD][:, DO:DO + 16:2], scalar1=-1.0, scalar2=None, op0=ALU.mult),
                      reads=(self.psr[psD],), writes=(r_sm[b],))
                sy.op("vector", lambda v, smb=smb: v.tensor_tensor(out=smb[:, 8:16], in0=self.ps[psD][:, DO:DO + 16:2], in1=smb[:, 0:8], op=ALU.max),
                      reads=(self.psr[psD], r_sm[b]), writes=(r_sm[b],))
                sy.op("vector", lambda v, thr=thr, smb=smb: v.tensor_tensor(out=smb[:, 16:24], in0=smb[:, 8:16], in1=thr, op=ALU.max),
                      reads=(r_sm[b], rg), writes=(r_sm[b],))
                sy.op("vector", lambda v, smb=smb: v.reciprocal(out=smb[:, 24:32], in_=smb[:, 16:24]), reads=(r_sm[b],), writes=(r_sm[b],))
                ho = hout[b]
                for hb_ in range(2):
                    sy.op("vector", lambda v, hb_=hb_, ho=ho, smb=smb: v.tensor_tensor(
                        out=ho[:, hb_ * 4:(hb_ + 1) * 4, :], in0=self.ps[psN[hb_]][:, :].rearrange("p (h e) -> p h e", h=4),
                        in1=smb[:, 24 + hb_ * 4:28 + hb_ * 4].unsqueeze(2).to_broadcast([128, 4, 128]), op=ALU.mult),
                        reads=(self.psr[psN[hb_]], r_sm[b]), writes=(r_hout[b],))
                yield
                for i4 in range(4):
                    for hh in range(2):
                        bank = psC[hh]
                        sy.mm(self.ps[bank][:, i4 * 128:(i4 + 1) * 128], [(kw[b][:, i4 * 128:(i4 + 1) * 128], vtc[lb][:, (2 * i4 + hh) * 128:(2 * i4 + hh + 1) * 128])],
                              reads=(r_kw[b], r_v[lb]), wreg=self.psr[bank] if i4 == 0 else Reg())
                for i4 in range(4):
                    sy.mm(self.ps[psD][:, DO + 16 + 2 * i4:DO + 18 + 2 * i4], [(kw[b][:, i4 * 128:(i4 + 1) * 128], self.ones_col[:, :])],
                          reads=(r_kw[b], self.cst), wreg=Reg())
                e = sy.eng["tensor"]
                for bk in (psD, psC[0], psC[1]):
                    self.psr[bk].w = {e["sem"]: e["cnt"]}
                yield
                for hh in range(2):
                    rows = slice(hh * 64, hh * 64 + 64)
                    esel = ebl[rows, hh::2]
                    sy.op("vector", lambda v, rows=rows, hh=hh, d=d: v.tensor_tensor(
                        out=Ctmp[d][rows, :, :], in0=self.ps[psC[hh]][rows, :].rearrange("p (i e) -> p i e", i=4), in1=C[d][rows, :, :], op=ALU.add),
                        reads=(self.psr[psC[hh]], r_C[d]), writes=(r_Ct[d],))
                    sy.op("vector", lambda v, rows=rows, esel=esel, d=d: v.tensor_tensor(
                        out=C[d][rows, :, :], in0=Ctmp[d][rows, :, :], in1=esel.unsqueeze(2).to_broadcast([64, 4, 128]), op=ALU.mult),
                        reads=(r_Ct[d], rg, r_C[d]), writes=(r_C[d],))
                    sy.op("vector", lambda v, rows=rows, d=d: v.tensor_tensor(
                        out=ntmp[d][rows, :], in0=self.ps[psD][rows, DO + 16:DO + 24:2], in1=nst[d][rows, :], op=ALU.add),
                        reads=(self.psr[psD], r_n[d]), writes=(r_nt[d],))
                    sy.op("vector", lambda v, rows=rows, esel=esel, d=d: v.tensor_tensor(
                        out=nst[d][rows, :], in0=ntmp[d][rows, :], in1=esel, op=ALU.mult),
                        reads=(r_nt[d], rg, r_n[d]), writes=(r_n[d],))
                for hh in range(2):
                    rows = slice(hh * 64, hh * 64 + 64)
                    sy.op("scalar", lambda a, rows=rows, hh=hh, d=d: a.copy(out=Cbf[d][rows, hh::2, :], in_=C[d][rows, :, :]), reads=(r_C[d],), writes=(r_Cbf[d],))
                    sy.op("scalar", lambda a, rows=rows, hh=hh, d=d: a.copy(out=nbf[d][rows, hh::2, :], in_=nst[d][rows, :].unsqueeze(2).to_broadcast([64, 4, 2])),
                          reads=(r_n[d],), writes=(r_nbf[d],))
                sy.dma("gpsimd", hdst[d][tok, :], ho[:, :, :].rearrange("p h e -> p (h e)"), reads=(r_hout[b],), writes=(r_h[d],))

        def lockstep(step):
            g0, g1 = body(step, 0), body(step, 1)
            a0 = a1 = True
            while a0 or a1:
                if a0:
                    try:
                        next(g0)
                    except StopIteration:
                        a0 = False
                if a1:
                    try:
                        next(g1)
                    except StopIteration:
                        a1 = False
            return
            yield
        for step in range(NCH):
            for _ in lockstep(step):
                pass
        self.phase_end()

    def mlstm_epilogue(self, i):
        sy = self.sy
        hg, rmod = self.m_hg, self.m_rmod
        self.phase_begin()
        wout = self.tsb("e_wout", [128, KC, D], BF16)
        ngb = self.tsb("e_ngb", [128, D])
        otc = [self.tsb(f"e_o{b}", [128, D]) for b in range(2)]
        hfc = [self.tsb(f"e_hf{b}", [128, 8, 128]) for b in range(2)]
        hbc = [self.tsb(f"e_hb{b}", [128, 8, 128]) for b in range(2)]
        xc = [self.tsb(f"e_x{b}", [128, KC, 128]) for b in range(2)]
        cen = [self.tsb(f"e_cen{b}", [128, 8, 128]) for b in range(2)]
        sq = self.tsb("e_sq", [128, 8, 128])
        ybf = [self.tsb(f"e_y{b}", [128, D], BF16) for b in range(2)]
        yT = [self.tsb(f"e_yT{b}", [128, D], BF16) for b in range(2)]
        sm = [self.tsb(f"e_sm{b}", [128, 32]) for b in range(2)]
        r_wout, r_ngb, r_sq = Reg(), Reg(), Reg()
        r_o, r_hf, r_hb, r_x, r_cen, r_y, r_yT, r_sm = ([Reg(), Reg()] for _ in range(8))
        r_xs = self.R("xT", id(self.xs))
        for k in range(KC):
            sy.dma("sync", wout[:, k, :], self.m_wouts[:, k, :], reads=(self.R("m_wouts"),), writes=(r_wout,))
        sy.dma("sync", ngb[:, :], self.m_norm_g, writes=(r_ngb,))
        xsv = self.xs.rearrange("(k p) t -> p k t", p=128)
        nctx = CTX // 128
        def body(c):
            b = c % 2
            tok = slice(c * 128, (c + 1) * 128)
            sy.dma("sync", otc[b][:, :], self.otok[tok, :], reads=(self.R("otok"),), writes=(r_o[b],))
            sy.dma("sync", hfc[b][:, :, :].rearrange("p h e -> p (h e)"), self.hfw[tok, :], reads=(self.R("hfw"),), writes=(r_hf[b],))
            sy.dma("sync", hbc[b][:, :, :].rearrange("p h e -> p (h e)"), self.hbw[tok, :], reads=(self.R("hbw"),), writes=(r_hb[b],))
            sy.dma("sync", xc[b][:, :, :], xsv[:, :, tok], reads=(r_xs,), writes=(r_x[b],))
            ho, smb, ce = hfc[b], sm[b], cen[b]
            yield
            sy.op("vector", lambda v, ho=ho, b=b: v.tensor_tensor(out=ho[:, :, :], in0=ho[:, :, :], in1=hbc[b][:, :, :], op=ALU.add),
                  reads=(r_hf[b], r_hb[b]), writes=(r_hf[b],))
            sy.op("vector", lambda v, ho=ho, smb=smb: v.tensor_reduce(out=smb[:, 0:8], in_=ho[:, :, :], axis=AX.X, op=ALU.add),
                  reads=(r_hf[b],), writes=(r_sm[b],))
            sy.op("vector", lambda v, smb=smb: v.tensor_scalar(out=smb[:, 8:16], in0=smb[:, 0:8], scalar1=1.0 / 128.0, scalar2=None, op0=ALU.mult),
                  reads=(r_sm[b],), writes=(r_sm[b],))
            sy.op("vector", lambda v, ho=ho, smb=smb, ce=ce: v.tensor_tensor(out=ce[:, :, :], in0=ho[:, :, :], in1=smb[:, 8:16].unsqueeze(2).to_broadcast([128, 8, 128]), op=ALU.subtract),
                  reads=(r_hf[b], r_sm[b]), writes=(r_cen[b],))
            yield
            sy.op("scalar", lambda a, ce=ce: a.activation(out=sq[:, :, :], in_=ce[:, :, :], func=AF.Square), reads=(r_cen[b],), writes=(r_sq,))
            sy.op("vector", lambda v, smb=smb: v.tensor_reduce(out=smb[:, 16:24], in_=sq[:, :, :], axis=AX.X, op=ALU.add),
                  reads=(r_sq,), writes=(r_sm[b],))
            sy.op("scalar", lambda a, smb=smb: a.activation(out=smb[:, 24:32], in_=smb[:, 16:24], func=AF.Ln, bias=EPS, scale=1.0 / 128.0),
                  reads=(r_sm[b],), writes=(r_sm[b],))
            sy.op("scalar", lambda a, smb=smb: a.activation(out=smb[:, 24:32], in_=smb[:, 24:32], func=AF.Exp, scale=-0.5),
                  reads=(r_sm[b],), writes=(r_sm[b],))
            yield
            sy.op("scalar", lambda a, b=b: a.activation(out=otc[b][:, :], in_=otc[b][:, :], func=AF.Sigmoid),
                  reads=(r_o[b],), writes=(r_o[b],))
            sy.op("vector", lambda v, smb=smb, ce=ce: v.tensor_tensor(out=ce[:, :, :], in0=ce[:, :, :], in1=smb[:, 24:32].unsqueeze(2).to_broadcast([128, 8, 128]), op=ALU.mult),
                  reads=(r_cen[b], r_sm[b]), writes=(r_cen[b],))
            cen2 = ce[:, :, :].rearrange("p h e -> p (h e)")
            sy.op("vector", lambda v, cen2=cen2: v.tensor_tensor(out=cen2, in0=cen2, in1=ngb[:, :], op=ALU.mult),
                  reads=(r_cen[b], r_ngb), writes=(r_cen[b],))
            sy.op("vector", lambda v, cen2=cen2, b=b: v.tensor_tensor(out=ybf[b][:, :], in0=cen2, in1=otc[b][:, :], op=ALU.mult),
                  reads=(r_cen[b], r_o[b]), writes=(r_y[b],))
            yield
            pbank = b
            pT = self.ps[pbank][:, :].bitcast(BF16)
            for k in range(KC):
                sy.op("tensor", lambda t, k=k, pT=pT, b=b: t.transpose(pT[:, k * 128:(k + 1) * 128], ybf[b][:, k * 128:(k + 1) * 128], self.ident_b[:, :]),
                      reads=(r_y[b], self.R("identb")), writes=(self.psr[pbank],) if k == 0 else (), inc=(k == KC - 1))
            e = sy.eng["tensor"]
            self.psr[pbank].w = {e["sem"]: e["cnt"]}
            sy.op("scalar", lambda a, pT=pT, b=b: a.copy(out=yT[b][:, :], in_=pT[:, :]), reads=(self.psr[pbank],), writes=(r_yT[b],))
            yield
            for oc in range(KC):
                bank = 2 + 2 * b + oc // 4
                col = slice((oc % 4) * 128, (oc % 4 + 1) * 128)
                sy.mm(self.ps[bank][:, col], [(wout[:, k, oc * 128:(oc + 1) * 128], yT[b][:, k * 128:(k + 1) * 128]) for k in range(KC)],
                      reads=(r_wout, r_yT[b]), wreg=self.psr[bank] if oc % 4 == 0 else Reg())
                if oc % 4 == 3:
                    e = sy.eng["tensor"]
                    self.psr[bank].w = {e["sem"]: e["cnt"]}
            yield
            v_ = 0 if c >= nctx else 1
            for oh_ in range(2):
                bank = 2 + 2 * b + oh_
                sy.op("vector", lambda v, bank=bank, oh_=oh_, v_=v_, b=b: v.tensor_tensor(
                    out=cen[b][:, oh_ * 4:(oh_ + 1) * 4, :], in0=self.ps[bank][:, :].rearrange("p (o t) -> p o t", o=4),
                    in1=hg[:, v_, oh_ * 4:(oh_ + 1) * 4].unsqueeze(2).to_broadcast([128, 4, 128]), op=ALU.mult),
                    reads=(self.psr[bank], rmod, r_cen[b]), writes=(r_cen[b],))
            sy.op("vector", lambda v, b=b: v.tensor_tensor(out=xc[b][:, :, :], in0=xc[b][:, :, :], in1=cen[b][:, :, :], op=ALU.add),
                  reads=(r_x[b], r_cen[b]), writes=(r_x[b],))
            sy.dma("gpsimd", xsv[:, :, tok], xc[b][:, :, :], reads=(r_x[b],), writes=(r_xs,))

        run_pipeline([(lambda c=c: body(c)) for c in range(NCH)], depth=2)
        self.phase_end()

    def precast_ret(self):
        sy = self.sy
        r = self.R("r_wins")
        for k in range(KC):
            sy.dma("gpsimd", self.r_wins[:, k, :], self.r_w_in[k * 128:(k + 1) * 128, :], writes=(r,))
        r2 = self.R("r_wouts")
        for k in range(16):
            sy.dma("gpsimd", self.r_wouts[:, k, :], self.r_w_out[k * 128:(k + 1) * 128, :], writes=(r2,))

    def ret_inproj(self, i):
        sy = self.sy
        gs, sh, hg, rmod = self.sub_mod(i, 1, 1.0)
        self.r_hg = hg
        self.r_rmod = rmod
        self.phase_begin()
        win = self.tsb("r_win", [128, KC, R_IN], BF16)
        xt = self.tsb("r_xt", [128, KC, 512])
        hT = self.tsb("r_hT", [128, KC, 512], BF16)
        qk = self.tsb("r_qk", [128, 2, D])
        cs = self.tsb("r_cs", [128, 2, 128])
        t1 = self.tsb("r_t1", [128, 4, 128])
        t2 = self.tsb("r_t2", [128, 4, 128])
        qkr = self.tsb("r_qkr", [128, 2, D], BF16)
        qkT = self.tsb("r_qkT", [128, 2, D], BF16)
        stv = self.tsb("r_stv", [128, 2048], BF16)
        stg = self.tsb("r_stg", [128, 2048])
        r_win, r_xt, r_hT, r_qk, r_cs, r_t1, r_t2, r_qkr, r_qkT, r_stv, r_stg = (Reg() for _ in range(11))
        r_xs = self.R("xT", id(self.xs))
        for k in range(KC):
            sy.dma("sync", win[:, k, :], self.r_wins[:, k, :], reads=(self.R("r_wins"),), writes=(r_win,))
        for (c0, w, kind) in self.pieces():
            for k in range(KC):
                sy.dma("sync", xt[:, k, 0:w], self.xs[k * 128:(k + 1) * 128, c0:c0 + w], reads=(r_xs,), writes=(r_xt,))
            self.modnorm_piece(xt, r_xt, 0, w, kind, gs, sh, rmod, hT, r_hT, 0)
            def body(t4, c0=c0, w=w):
                c = c0 // 128 + t4
                tsl = slice(t4 * 128, (t4 + 1) * 128)
                tok = slice(c * 128, (c + 1) * 128)
                sy.dma("sync", cs[:, :, :], self.rope_cs[tok, :, :], writes=(r_cs,))
                for n2 in range(4):
                    pb = n2 % 4
                    sy.mm(self.ps[pb][:, :], [(hT[:, k, tsl], win[:, k, n2 * 512:(n2 + 1) * 512]) for k in range(KC)],
                          reads=(r_win, r_hT), wreg=self.psr[pb])
                    if n2 < 4:
                        sy.op("scalar", lambda a, pb=pb, n2=n2: a.copy(out=qk[:, n2 // 2, (n2 % 2) * 512:(n2 % 2 + 1) * 512], in_=self.ps[pb][:, :]),
                              reads=(self.psr[pb],), writes=(r_qk,))
                    elif n2 < 8:
                        sy.op("vector", lambda v, pb=pb, n2=n2: v.tensor_copy(out=stv[:, (n2 - 4) * 512:(n2 - 3) * 512], in_=self.ps[pb][:, :]),
                              reads=(self.psr[pb],), writes=(r_stv,))
                    else:
                        sy.op("scalar", lambda a, pb=pb, n2=n2: a.copy(out=stg[:, (n2 - 8) * 512:(n2 - 7) * 512], in_=self.ps[pb][:, :]),
                              reads=(self.psr[pb],), writes=(r_stg,))
                yield
                cosb = cs[:, 0, :].unsqueeze(1).to_broadcast([128, 4, 128])
                sinb = cs[:, 1, :].unsqueeze(1).to_broadcast([128, 4, 128])
                for a_ in range(2):
                    src = qk[:, a_, :].rearrange("p (h j t) -> p h j t", h=4, t=2)
                    dst = qkr[:, a_, :].rearrange("p (h j t) -> p h j t", h=4, t=2)
                    e_, o_ = src[:, :, :, 0], src[:, :, :, 1]
                    sy.op("vector", lambda v, e_=e_: v.tensor_tensor(out=t1[:, :, :], in0=e_, in1=cosb, op=ALU.mult), reads=(r_qk, r_cs), writes=(r_t1,))
                    sy.op("vector", lambda v, o_=o_: v.tensor_tensor(out=t2[:, :, :], in0=o_, in1=sinb, op=ALU.mult), reads=(r_qk, r_cs), writes=(r_t2,))
                    sy.op("vector", lambda v, dst=dst: v.tensor_tensor(out=dst[:, :, :, 0], in0=t1[:, :, :], in1=t2[:, :, :], op=ALU.subtract), reads=(r_t1, r_t2), writes=(r_qkr,))
                    sy.op("vector", lambda v, e_=e_: v.tensor_tensor(out=t1[:, :, :], in0=e_, in1=sinb, op=ALU.mult), reads=(r_qk, r_cs), writes=(r_t1,))
                    sy.op("vector", lambda v, o_=o_: v.tensor_tensor(out=t2[:, :, :], in0=o_, in1=cosb, op=ALU.mult), reads=(r_qk, r_cs), writes=(r_t2,))
                    sy.op("vector", lambda v, dst=dst: v.tensor_tensor(out=dst[:, :, :, 1], in0=t1[:, :, :], in1=t2[:, :, :], op=ALU.add), reads=(r_t1, r_t2), writes=(r_qkr,))
                yield
                sy.dma("gpsimd", self.rktok[tok, :], qkr[:, 1, :], reads=(r_qkr,), writes=(self.R("rktok"),))
                for a_ in range(2):
                    pT = self.ps[4 + a_][:, :].bitcast(BF16)
                    for j in range(8):
                        sy.op("tensor", lambda t, j=j, a_=a_, pT=pT: t.transpose(pT[:, j * 128:(j + 1) * 128], qkr[:, a_, j * 128:(j + 1) * 128], self.ident_b[:, :]),
                              reads=(r_qkr, self.R("identb")), writes=(self.psr[4 + a_],) if j == 0 else (), inc=(j == 7))
                    e = sy.eng["tensor"]
                    self.psr[4 + a_].w = {e["sem"]: e["cnt"]}
                    sy.op("scalar", lambda a, a_=a_, pT=pT: a.copy(out=qkT[:, a_, :], in_=pT[:, :]), reads=(self.psr[4 + a_],), writes=(r_qkT,))
                sy.dma("gpsimd", self.rqT[c], qkT[:, 0, :], reads=(r_qkT,), writes=(self.R("rqT"),))
                sy.dma("gpsimd", self.rkT[c], qkT[:, 1, :], reads=(r_qkT,), writes=(self.R("rkT"),))
                yield
                for n2 in range(4, 8):
                    pb = n2 % 4
                    sy.mm(self.ps[pb][:, :], [(hT[:, k, tsl], win[:, k, n2 * 512:(n2 + 1) * 512]) for k in range(KC)],
                          reads=(r_win, r_hT), wreg=self.psr[pb])
                    sy.op("vector", lambda v, pb=pb, n2=n2: v.tensor_copy(out=stv[:, (n2 - 4) * 512:(n2 - 3) * 512], in_=self.ps[pb][:, :]),
                          reads=(self.psr[pb],), writes=(r_stv,))
                sy.dma("gpsimd", self.rvtok[tok, :], stv[:, :], reads=(r_stv,), writes=(self.R("rvtok"),))
                yield
                for n2 in range(8, 12):
                    pb = n2 % 4
                    sy.mm(self.ps[pb][:, :], [(hT[:, k, tsl], win[:, k, n2 * 512:(n2 + 1) * 512]) for k in range(KC)],
                          reads=(r_win, r_hT), wreg=self.psr[pb])
                    sy.op("scalar", lambda a, pb=pb, n2=n2: a.copy(out=stg[:, (n2 - 8) * 512:(n2 - 7) * 512], in_=self.ps[pb][:, :]),
                          reads=(self.psr[pb],), writes=(r_stg,))
                sy.dma("gpsimd", self.rgtok[tok, :], stg[:, :], reads=(r_stg,), writes=(self.R("rgtok"),))

            run_pipeline([(lambda t4=t4: body(t4)) for t4 in range(w // 128)], depth=2, stagger=True)
        self.phase_end()

    def ret_consts(self, tsb_prefix):
        sy = self.sy
        dec = self.tsb(tsb_prefix + "dec", [128, 8])
        lgn = self.tsb(tsb_prefix + "lgn", [128, 8])
        pos = self.tsb(tsb_prefix + "pos", [128, 4])
        xi = self.tsb(tsb_prefix + "xi", [128, 8])
        kap = self.tsb(tsb_prefix + "kap", [128, 8])
        kapg = self.tsb(tsb_prefix + "kapg", [128, 8])
        g128 = self.tsb(tsb_prefix + "g128", [128, 8])
        kmask = self.tsb(tsb_prefix + "kmask", [128, 2, 4, 128])
        r_cst = Reg()
        sy.dma("sync", dec[:, :], self.r_decay, writes=(r_cst,))
        sy.dma("sync", pos[:, :], self.r_pos, writes=(r_cst,))
        sy.op("scalar", lambda a: a.activation(out=lgn[:, :], in_=dec[:, :], func=AF.Exp, scale=-1.0), reads=(r_cst,), writes=(r_cst,))
        sy.op("scalar", lambda a: a.activation(out=lgn[:, :], in_=lgn[:, :], func=AF.Ln, bias=1.0, scale=1.0), reads=(r_cst,), writes=(r_cst,))
        for d in range(2):
            for h in range(4):
                j = d * 4 + h
                sy.op("scalar", lambda a, j=j, d=d: a.activation(out=xi[:, j:j + 1], in_=pos[:, d:d + 1], func=AF.Exp, scale=lgn[:, j:j + 1]),
                      reads=(r_cst,), writes=(r_cst,))
                sy.op("scalar", lambda a, j=j, d=d: a.activation(out=kap[:, j:j + 1], in_=pos[:, 2 + d:3 + d], func=AF.Exp, scale=lgn[:, j:j + 1],
                                                               bias=float(np.log(1.0 / 16.0))),
                      reads=(r_cst,), writes=(r_cst,))
        sy.op("scalar", lambda a: a.activation(out=g128[:, :], in_=lgn[:, :], func=AF.Exp, scale=-128.0), reads=(r_cst,), writes=(r_cst,))
        sy.op("vector", lambda v: v.tensor_tensor(out=kapg[:, :], in0=kap[:, :], in1=g128[:, :], op=ALU.mult), reads=(r_cst,), writes=(r_cst,))
        for d in range(2):
            for h in range(4):
                j = d * 4 + h
                sy.op("vector", lambda v, d=d, h=h, j=j: v.tensor_scalar(out=kmask[:, d, h, :], in0=self.masks[:, d, :], scalar1=kap[:, j:j + 1], scalar2=None, op0=ALU.mult),
                      reads=(r_cst, self.cst), writes=(r_cst,))
        return xi, kapg, g128, kmask, r_cst

    def ret_scan(self, i):
        sy = self.sy
        self.phase_begin()
        xi, kapg, g128, kmask, r_cst = self.ret_consts("t_")
        qTc = [self.tsb(f"t_q{b}", [128, 8, 128], BF16) for b in range(4)]
        kTc = [self.tsb(f"t_k{b}", [128, 8, 128], BF16) for b in range(4)]
        ktc = [self.tsb(f"t_kt{b}", [128, D], BF16) for b in range(4)]
        vtc = [self.tsb(f"t_v{b}", [128, 2048], BF16) for b in range(4)]
        kh = [self.tsb(f"t_kh{b}", [128, D], BF16) for b in range(2)]
        Pp = [self.tsb(f"t_P{b}", [128, 4, 128], BF16) for b in range(2)]
        oh = [self.tsb(f"t_oh{b}", [128, 4, 512]) for b in range(2)]
        Rs = [self.tsb(f"t_R{d}", [128, 8, 512]) for d in range(2)]
        Rbf = [self.tsb(f"t_Rbf{d}", [128, 8, 512], BF16) for d in range(2)]
        r_kh, r_P, r_oh, r_R, r_Rbf = ([Reg(), Reg()] for _ in range(5))
        r_q, r_k, r_kt, r_v = ([Reg() for _ in range(4)] for _ in range(4))
        r_o = [self.R("rofw"), self.R("robw")]
        odst = [self.rofw, self.robw]
        nctx = CTX // 128
        orders = [list(range(NCH)), list(range(nctx - 1, -1, -1)) + list(range(NCH - 1, nctx - 1, -1))]
        for d in range(2):
            sy.op("vector", lambda v, d=d: v.memset(Rs[d][:, :, :], 0.0), writes=(r_R[d],))
            sy.op("vector", lambda v, d=d: v.memset(Rbf[d][:, :, :], 0.0), writes=(r_Rbf[d],))
        def body(step, d):
            if True:
                c = orders[d][step]
                b = d
                pS, pO, pR = 4 * d, 4 * d + 1, (4 * d + 2, 4 * d + 3)
                lb = 2 * d + step % 2
                tok = slice(c * 128, (c + 1) * 128)
                sy.dma("sync", qTc[lb][:, :, :], self.rqT[c].rearrange("p (j t) -> p j t", j=8), reads=(self.R("rqT"),), writes=(r_q[lb],))
                sy.dma("sync", kTc[lb][:, :, :], self.rkT[c].rearrange("p (j t) -> p j t", j=8), reads=(self.R("rkT"),), writes=(r_k[lb],))
                sy.dma("sync", ktc[lb][:, :], self.rktok[tok, :], reads=(self.R("rktok"),), writes=(r_kt[lb],))
                sy.dma("sync", vtc[lb][:, :], self.rvtok[tok, :], reads=(self.R("rvtok"),), writes=(r_v[lb],))
                yield
                sy.op("vector", lambda v, b=b, d=d: v.tensor_tensor(
                    out=kh[b][:, :].rearrange("p (h e) -> p h e", h=4), in0=ktc[lb][:, :].rearrange("p (h e) -> p h e", h=4),
                    in1=kapg[:, d * 4:(d + 1) * 4].unsqueeze(2).to_broadcast([128, 4, 256]), op=ALU.mult),
                    reads=(r_kt[lb], r_cst), writes=(r_kh[b],))
                if c >= nctx:
                    for h in range(4):
                        sy.mm(self.ps[pS][:, h * 128:(h + 1) * 128], [(kTc[lb][:, 2 * h + dc, :], qTc[lb][:, 2 * h + dc, :]) for dc in range(2)],
                              reads=(r_k[lb], r_q[lb]), wreg=self.psr[pS] if h == 0 else Reg())
                    e = sy.eng["tensor"]
                    self.psr[pS].w = {e["sem"]: e["cnt"]}
                    yield
                    sy.op("vector", lambda v, d=d, b=b: v.tensor_tensor(out=Pp[b][:, :, :], in0=self.ps[pS][:, :].rearrange("p (h t) -> p h t", h=4), in1=kmask[:, d, :, :], op=ALU.mult),
                          reads=(self.psr[pS], r_cst), writes=(r_P[b],))
                    yield
                    for h in range(4):
                        bank = pO
                        sy.mm(self.ps[bank][:, :], [(qTc[lb][:, 2 * h + dc, :], Rbf[d][:, 2 * h + dc, :]) for dc in range(2)] + [(Pp[b][:, h, :], vtc[lb][:, h * 512:(h + 1) * 512])],
                              reads=(r_q[lb], r_Rbf[d], r_P[b], r_v[lb]), wreg=self.psr[bank])
                        sy.op("scalar", lambda a, h=h, bank=bank, d=d, b=b: a.activation(out=oh[b][:, h, :], in_=self.ps[bank][:, :], func=AF.Copy, scale=xi[:, d * 4 + h:d * 4 + h + 1]),
                              reads=(self.psr[bank], r_cst), writes=(r_oh[b],))
                    sy.dma("gpsimd", odst[d][tok, :], oh[b][:, :, :].rearrange("p h e -> p (h e)"), reads=(r_oh[b],), writes=(r_o[d],))
                for h in range(4):
                    yield
                    for dc in range(2):
                        j = 2 * h + dc
                        bank = pR[j % 2]
                        sy.mm(self.ps[bank][:, :], [(kh[b][:, j * 128:(j + 1) * 128], vtc[lb][:, h * 512:(h + 1) * 512])], reads=(r_kh[b], r_v[lb]), wreg=self.psr[bank])
                        sy.op("vector", lambda v, j=j, h=h, d=d, bank=bank: v.scalar_tensor_tensor(
                            out=Rs[d][:, j, :], in0=Rs[d][:, j, :], scalar=g128[:, d * 4 + h:d * 4 + h + 1], in1=self.ps[bank][:, :], op0=ALU.mult, op1=ALU.add),
                            reads=(self.psr[bank], r_cst, r_R[d]), writes=(r_R[d],))
                        sy.op("scalar", lambda a, j=j, d=d: a.copy(out=Rbf[d][:, j, :], in_=Rs[d][:, j, :]), reads=(r_R[d],), writes=(r_Rbf[d],))

        for step in range(NCH):
            g0, g1 = body(step, 0), body(step, 1)
            a0 = a1 = True
            while a0 or a1:
                if a0:
                    try:
                        next(g0)
                    except StopIteration:
                        a0 = False
                if a1:
                    try:
                        next(g1)
                    except StopIteration:
                        a1 = False
        self.phase_end()

    def ret_epilogue(self, i):
        sy = self.sy
        hg, rmod = self.r_hg, self.r_rmod
        self.phase_begin()
        wout = self.tsb("u_wout", [128, 16, D], BF16)
        ngb = self.tsb("u_ngb", [128, 2048])
        gtc = [self.tsb(f"u_g{b}", [128, 2048]) for b in range(2)]
        ofc = [self.tsb(f"u_of{b}", [128, 4, 512]) for b in range(2)]
        obc = [self.tsb(f"u_ob{b}", [128, 4, 512]) for b in range(2)]
        xc = [self.tsb(f"u_x{b}", [128, KC, 128]) for b in range(2)]
        sq = self.tsb("u_sq", [128, 4, 512])
        ybf = [self.tsb(f"u_y{b}", [128, 2048], BF16) for b in range(2)]
        yT = [self.tsb(f"u_yT{b}", [128, 2048], BF16) for b in range(2)]
        sm = [self.tsb(f"u_sm{b}", [128, 32]) for b in range(2)]
        upd = self.tsb("u_upd", [128, KC, 128])
        r_wout, r_ngb, r_sq, r_upd = (Reg() for _ in range(4))
        r_g, r_of, r_ob, r_x, r_y, r_yT, r_sm = ([Reg(), Reg()] for _ in range(7))
        r_xs = self.R("xT", id(self.xs))
        for k in range(16):
            sy.dma("sync", wout[:, k, :], self.r_wouts[:, k, :], reads=(self.R("r_wouts"),), writes=(r_wout,))
        sy.dma("sync", ngb[:, :], self.r_norm_g, writes=(r_ngb,))
        xsv = self.xs.rearrange("(k p) t -> p k t", p=128)
        nctx = CTX // 128
        def body(c):
            b = c % 2
            tok = slice(c * 128, (c + 1) * 128)
            sy.dma("sync", gtc[b][:, :], self.rgtok[tok, :], reads=(self.R("rgtok"),), writes=(r_g[b],))
            sy.dma("sync", ofc[b][:, :, :].rearrange("p h e -> p (h e)"), self.rofw[tok, :], reads=(self.R("rofw"),), writes=(r_of[b],))
            sy.dma("sync", obc[b][:, :, :].rearrange("p h e -> p (h e)"), self.robw[tok, :], reads=(self.R("robw"),), writes=(r_ob[b],))
            sy.dma("sync", xc[b][:, :, :], xsv[:, :, tok], reads=(r_xs,), writes=(r_x[b],))
            o_, smb = ofc[b], sm[b]
            cen = obc[b]
            r_cen = r_ob[b]
            yield
            sy.op("vector", lambda v, o_=o_, b=b: v.tensor_tensor(out=o_[:, :, :], in0=o_[:, :, :], in1=obc[b][:, :, :], op=ALU.add), reads=(r_of[b], r_ob[b]), writes=(r_of[b],))
            sy.op("vector", lambda v, o_=o_, smb=smb: v.tensor_reduce(out=smb[:, 0:4], in_=o_[:, :, :], axis=AX.X, op=ALU.add), reads=(r_of[b],), writes=(r_sm[b],))
            sy.op("vector", lambda v, smb=smb: v.tensor_scalar(out=smb[:, 4:8], in0=smb[:, 0:4], scalar1=1.0 / 512.0, scalar2=None, op0=ALU.mult), reads=(r_sm[b],), writes=(r_sm[b],))
            sy.op("vector", lambda v, o_=o_, smb=smb: v.tensor_tensor(out=cen[:, :, :], in0=o_[:, :, :], in1=smb[:, 4:8].unsqueeze(2).to_broadcast([128, 4, 512]), op=ALU.subtract),
                  reads=(r_of[b], r_sm[b]), writes=(r_cen,))
            yield
            sy.op("scalar", lambda a: a.activation(out=sq[:, :, :], in_=cen[:, :, :], func=AF.Square), reads=(r_cen,), writes=(r_sq,))
            sy.op("vector", lambda v, smb=smb: v.tensor_reduce(out=smb[:, 8:12], in_=sq[:, :, :], axis=AX.X, op=ALU.add), reads=(r_sq,), writes=(r_sm[b],))
            sy.op("scalar", lambda a, smb=smb: a.activation(out=smb[:, 12:16], in_=smb[:, 8:12], func=AF.Ln, bias=EPS, scale=1.0 / 512.0), reads=(r_sm[b],), writes=(r_sm[b],))
            sy.op("scalar", lambda a, smb=smb: a.activation(out=smb[:, 12:16], in_=smb[:, 12:16], func=AF.Exp, scale=-0.5), reads=(r_sm[b],), writes=(r_sm[b],))
            yield
            sy.op("scalar", lambda a, b=b: a.activation(out=gtc[b][:, :], in_=gtc[b][:, :], func=AF.Silu), reads=(r_g[b],), writes=(r_g[b],))
            sy.op("vector", lambda v, smb=smb: v.tensor_tensor(out=cen[:, :, :], in0=cen[:, :, :], in1=smb[:, 12:16].unsqueeze(2).to_broadcast([128, 4, 512]), op=ALU.mult),
                  reads=(r_cen, r_sm[b]), writes=(r_cen,))
            cen2 = cen[:, :, :].rearrange("p h e -> p (h e)")
            sy.op("vector", lambda v, cen2=cen2: v.tensor_tensor(out=cen2, in0=cen2, in1=ngb[:, :], op=ALU.mult), reads=(r_cen, r_ngb), writes=(r_cen,))
            sy.op("vector", lambda v, cen2=cen2, b=b: v.tensor_tensor(out=ybf[b][:, :], in0=cen2, in1=gtc[b][:, :], op=ALU.mult), reads=(r_cen, r_g[b]), writes=(r_y[b],))
            yield
            for half in range(2):
                pbank = 2 * b + half
                pT = self.ps[pbank][:, :].bitcast(BF16)
                for k in range(8):
                    kk = half * 8 + k
                    sy.op("tensor", lambda t, k=k, kk=kk, pT=pT, b=b: t.transpose(pT[:, k * 128:(k + 1) * 128], ybf[b][:, kk * 128:(kk + 1) * 128], self.ident_b[:, :]),
                          reads=(r_y[b], self.R("identb")), writes=(self.psr[pbank],) if k == 0 else (), inc=(k == 7))
                e = sy.eng["tensor"]
                self.psr[pbank].w = {e["sem"]: e["cnt"]}
                sy.op("scalar", lambda a, half=half, pT=pT, b=b: a.copy(out=yT[b][:, half * 1024:(half + 1) * 1024], in_=pT[:, :]), reads=(self.psr[pbank],), writes=(r_yT[b],))
            yield
            for oc in range(KC):
                bank = 4 + 2 * b + oc // 4
                col = slice((oc % 4) * 128, (oc % 4 + 1) * 128)
                sy.mm(self.ps[bank][:, col], [(wout[:, k, oc * 128:(oc + 1) * 128], yT[b][:, k * 128:(k + 1) * 128]) for k in range(16)],
                      reads=(r_wout, r_yT[b]), wreg=self.psr[bank] if oc % 4 == 0 else Reg())
                if oc % 4 == 3:
                    e = sy.eng["tensor"]
                    self.psr[bank].w = {e["sem"]: e["cnt"]}
            yield
            for oh_ in range(2):
                bank = 4 + 2 * b + oh_
                sy.op("vector", lambda v, bank=bank, oh_=oh_: v.tensor_tensor(
                    out=upd[:, oh_ * 4:(oh_ + 1) * 4, :], in0=self.ps[bank][:, :].rearrange("p (o t) -> p o t", o=4),
                    in1=hg[:, 0, oh_ * 4:(oh_ + 1) * 4].unsqueeze(2).to_broadcast([128, 4, 128]), op=ALU.mult),
                    reads=(self.psr[bank], rmod), writes=(r_upd,))
            sy.op("vector", lambda v, b=b: v.tensor_tensor(out=xc[b][:, :, :], in0=xc[b][:, :, :], in1=upd[:, :, :], op=ALU.add),
                  reads=(r_x[b], r_upd), writes=(r_x[b],))
            sy.dma("gpsimd", xsv[:, :, tok], xc[b][:, :, :], reads=(r_x[b],), writes=(r_xs,))

        run_pipeline([(lambda c=c: body(c)) for c in range(nctx, NCH)], depth=2)
        self.phase_end()

    def dump_xs(self):
        dbg = self.nc.dram_tensor("dbgx", [D, NT], F32, kind="ExternalOutput").ap()
        r = self.R("dbgx")
        self.phase_begin()
        t = self.tsb("dump_t", [128, 2048])
        rt = Reg()
        for k in range(KC):
            for c0 in range(0, NT, 2048):
                w = min(2048, NT - c0)
                self.sy.dma("sync", t[:, 0:w], self.xs[k * 128:(k + 1) * 128, c0:c0 + w], reads=(self.R("xT", id(self.xs)),), writes=(rt,))
                self.sy.dma("sync", dbg[k * 128:(k + 1) * 128, c0:c0 + w], t[:, 0:w], reads=(rt,), writes=(r,))
        self.phase_end()
        self.sy.finish([r])

    def build(self):
        for (i, j) in ((0, 0), (0, 1), (1, 0), (1, 1)):
            if self.stop_after != "mod":
                self.precast_ffn(i, j)
        if self.stop_after == "mod":
            self.compute_mod((0, 1))
        else:
            self.compute_mod((0,))
        if self.stop_after == "mod":
            dbg = self.nc.dram_tensor("dbg", [128, 4, NMOD * KC], F32, kind="ExternalOutput").ap()
            r = self.R("dbg")
            self.sy.dma("sync", dbg[:, 0:2, :], self.modx[:, :, :], reads=(self.r_mod,), writes=(r,))
            self.sy.dma("sync", dbg[:, 2:4, :], self.modc[:, :, :], reads=(self.r_mod,), writes=(r,))
            self.sy.finish([r])
            return self.nc
        if self.stop_after == "ffn00":
            self.ffn(0, 0, self.xT_in, final=True)
            self.sy.finish([self.R("outT")])
            return self.nc
        self.precast_mlstm()
        self.ffn(0, 0, self.xT_in)
        self.mlstm_inproj(0)
        if self.stop_after == "minproj":
            self.dump_xs()
            return self.nc
        self.compute_mod((1,))
        self.mlstm_conv()
        if self.stop_after == "mconv":
            self.dump_xs()
            return self.nc
        self.mlstm_scan(0)
        if self.stop_after == "mscan":
            self.dump_xs()
            return self.nc
        self.mlstm_epilogue(0)
        if self.stop_after == "mlstm":
            self.dump_xs()
            return self.nc
        self.precast_ret()
        self.ffn(0, 1, self.xs)
        self.ffn(1, 0, self.xs)
        if self.stop_after == "ffn10":
            self.dump_xs()
            return self.nc
        self.ret_inproj(1)
        if self.stop_after == "rinproj":
            self.dump_xs()
            return self.nc
        self.ret_scan(1)
        if self.stop_after == "rscan":
            self.dump_xs()
            return self.nc
        self.ret_epilogue(1)
        if self.stop_after == "ret":
            self.dump_xs()
            return self.nc
        self.ffn(1, 1, self.xs, final=True)
        self.sy.finish([self.R("outT")])
        return self.nc


def host_layout(inp, b):
    f32 = np.float32
    x = np.asarray(inp["x"][b], f32)
    ctx = np.asarray(inp["ctx"][b], f32)
    xT = np.ascontiguousarray(np.concatenate([ctx, x], axis=0).T)
    cc = np.stack([np.asarray(inp["c"][b], f32), np.asarray(inp["c_ctx"], f32)], axis=-1)
    cT = np.ascontiguousarray(cc.reshape(KC, 128, 2).transpose(1, 0, 2))
    mod_b = np.ascontiguousarray(np.asarray(inp["mod_b"], f32).reshape(2, NMOD * KC, 128).transpose(0, 2, 1))
    norm_g = np.ascontiguousarray(np.asarray(inp["norm_g"], f32).reshape(2 * 3 * KC, 128).T)
    final_g = np.ascontiguousarray(np.asarray(inp["final_g"], f32).reshape(KC, 128).T)
    m = {
        "xT": xT, "cT": cT, "mod_w": np.asarray(inp["mod_w"], f32), "mod_b": mod_b, "norm_g": norm_g,
        "ffn_w13": np.asarray(inp["ffn_w13"], f32), "ffn_w2": np.asarray(inp["ffn_w2"], f32),
        "final_g": final_g, "ident": np.eye(128, dtype=f32),
        "m_w_in": np.asarray(inp["m_w_in"][0], f32), "m_w_out": np.asarray(inp["m_w_out"][0], f32),
        "m_gate_b": np.ascontiguousarray(np.broadcast_to(np.asarray(inp["m_gate_b"][0], f32)[None, :], (128, 32))),
        "m_conv_w": np.ascontiguousarray(np.asarray(inp["m_conv_w"][0], f32).reshape(5, KC, 128).transpose(2, 1, 0)),
        "m_norm_g": np.ascontiguousarray(np.broadcast_to(np.asarray(inp["m_norm_g"][0], f32)[None, :], (128, D))),
        "masks": MASKS,
        "r_w_in": np.asarray(inp["r_w_in"][0], f32), "r_w_out": np.asarray(inp["r_w_out"][0], f32),
        "r_decay": np.ascontiguousarray(np.broadcast_to(np.asarray(inp["r_decay"][0], f32).reshape(1, 8), (128, 8))),
        "r_norm_g": np.ascontiguousarray(np.broadcast_to(np.asarray(inp["r_norm_g"][0], f32)[None, :], (128, 2048))),
        "r_pos": R_POS, "rope_cs": rope_table(),
    }
    return m


_tri = np.triu(np.ones((128, 128), np.float32))
MASKS = np.ascontiguousarray(np.stack([_tri, _tri.T, -_tri, -_tri.T], axis=0))

_p = np.arange(128, dtype=np.float32)
R_POS = np.ascontiguousarray(np.stack([-(_p + 1), -(128 - _p), (_p + 1), (128 - _p)], axis=1))


def rope_table():
    n_f = 64
    inv = np.power(np.float32(10000.0), -np.arange(n_f, dtype=np.float32) / np.float32(n_f)).astype(np.float32)
    t = np.arange(SEQ)
    row = (t // 64).astype(np.float32)
    col = (t % 64).astype(np.float32)
    ang = np.concatenate([row[:, None] * inv, col[:, None] * inv], axis=-1).astype(np.float32)
    tab = np.zeros((NT, 2, 128), np.float32)
    tab[:CTX, 0] = 1.0
    tab[CTX:, 0] = np.cos(ang)
    tab[CTX:, 1] = np.sin(ang)
    return tab


_CACHE = {}


def kernel(**inputs):
    if "nc" not in _CACHE:
        _CACHE["nc"] = Builder().build()
    nc = _CACHE["nc"]
    in_maps = [host_layout(inputs, b) for b in range(N_CORES)]
    res = run_bass_kernel_spmd(nc, in_maps, core_ids=list(range(N_CORES)))
    out = np.stack([np.ascontiguousarray(res.results[b]["outT"].T) for b in range(N_CORES)], axis=0)
    return out.astype(np.float32)
```
